# Optimizing a Trainium2 kernel written in Bass

```python
import math
import jax, jax.numpy as jnp
from jax import lax
import numpy as np

D_MODEL = 2048
BATCH = 1
SEQ = 8192
DEPTH = 4

BLOCK_Q = 128
RMS_EPS = 1e-6
ROPE_THETA = 10000.0
MASK_VALUE = -1e30

DA_HEADS = 8
DA_QK_DIM = 128
DA_V_DIM = 2 * DA_QK_DIM
DA_WIDTH = DA_HEADS * DA_V_DIM
LAMBDA_STD = 0.1

SB_HEADS = 16
SB_HEAD_DIM = 128
SB_WIDTH = SB_HEADS * SB_HEAD_DIM

SEGMENT_SIZES = (
    2 * DA_HEADS * DA_QK_DIM,
    2 * DA_HEADS * DA_QK_DIM,
    DA_WIDTH,
    DA_WIDTH,
    SB_WIDTH,
    SB_WIDTH,
    SB_WIDTH,
    SB_WIDTH,
    D_MODEL,
    D_MODEL,
)
IN_COLS = sum(SEGMENT_SIZES)

kernel_name = "hybrid_diffattn_stickbreaking_gated"


def rms_norm(x, gain):
    xf = x.astype(jnp.float32)
    y = xf * lax.rsqrt(jnp.mean(xf * xf, axis=-1, keepdims=True) + RMS_EPS)
    return (y * gain.astype(jnp.float32)).astype(x.dtype)


def rope(x):
    s_len, d = x.shape[1], x.shape[-1]
    inv_freq = jnp.exp(-(jnp.arange(0, d, 2, dtype=jnp.float32) / d) * math.log(ROPE_THETA))
    ang = jnp.arange(s_len, dtype=jnp.float32)[:, None] * inv_freq[None, :]
    cos = jnp.cos(ang)[None, :, None, :]
    sin = jnp.sin(ang)[None, :, None, :]
    xf = x.astype(jnp.float32)
    x1, x2 = xf[..., : d // 2], xf[..., d // 2:]
    return jnp.concatenate([x1 * cos - x2 * sin, x2 * cos + x1 * sin], axis=-1).astype(x.dtype)


def to_query_blocks(t):
    b, s, h, d = t.shape
    return t.reshape(b, s // BLOCK_Q, BLOCK_Q, h, d).transpose(1, 0, 3, 2, 4)


def from_query_blocks(o):
    nb, b, h, bq, dv = o.shape
    return o.transpose(1, 0, 3, 2, 4).reshape(b, nb * bq, h * dv)


def differential_attention(q1, q2, k1, k2, v, lam):
    s_len = q1.shape[1]
    k1t = k1.transpose(0, 2, 1, 3)
    k2t = k2.transpose(0, 2, 1, 3)
    vt = v.transpose(0, 2, 1, 3)
    kpos = jnp.arange(s_len, dtype=jnp.int32)
    qpos = kpos.reshape(s_len // BLOCK_Q, BLOCK_Q)

    def block(args):
        qb1, qb2, qp = args
        causal = kpos[None, :] <= qp[:, None]

        def probs(qb, kt):
            s = jnp.einsum('bhqd,bhkd->bhqk', qb, kt, preferred_element_type=jnp.float32)
            return jax.nn.softmax(jnp.where(causal, s, MASK_VALUE), axis=-1)

        a = probs(qb1, k1t) - lam * probs(qb2, k2t)
        return jnp.einsum('bhqk,bhkd->bhqd', a.astype(vt.dtype), vt)

    out = lax.map(block, (to_query_blocks(q1), to_query_blocks(q2), qpos))
    return from_query_blocks(out)


def stick_breaking_attention(q, k, v):
    s_len = q.shape[1]
    kt = k.transpose(0, 2, 1, 3)
    vt = v.transpose(0, 2, 1, 3)
    kpos = jnp.arange(s_len, dtype=jnp.int32)
    qpos = kpos.reshape(s_len // BLOCK_Q, BLOCK_Q)

    def block(args):
        qb, qp = args
        strict = kpos[None, :] < qp[:, None]
        z = jnp.einsum('bhqd,bhkd->bhqk', qb, kt, preferred_element_type=jnp.float32)
        log_fail = jnp.where(strict, jax.nn.log_sigmoid(-z), 0.0)
        after = lax.cumsum(log_fail, axis=3, reverse=True) - log_fail
        w = jnp.where(strict, jnp.exp(jax.nn.log_sigmoid(z) + after), 0.0)
        return jnp.einsum('bhqk,bhkd->bhqd', w.astype(vt.dtype), vt)

    out = lax.map(block, (to_query_blocks(q), qpos))
    return from_query_blocks(out)


def split_columns(proj):
    parts, start = [], 0
    for size in SEGMENT_SIZES:
        parts.append(proj[..., start:start + size])
        start += size
    return parts


def hybrid_layer(x, layer_idx, norm_gain, w_in, qk_q_gain, qk_k_gain,
                 lambda_q1, lambda_k1, lambda_q2, lambda_k2, subln_gain,
                 w_branch_a, w_branch_b, w_out):
    b, s, _ = x.shape
    h = rms_norm(x, norm_gain)
    proj = jnp.einsum('bsd,de->bse', h, w_in)
    (da_q, da_k, da_v, da_z, sb_q, sb_k, sb_v, sb_z,
     gate_a, gate_b) = split_columns(proj)

    lambda_init = 0.8 - 0.6 * math.exp(-0.3 * layer_idx)
    q = rope(rms_norm(da_q.reshape(b, s, 2 * DA_HEADS, DA_QK_DIM), qk_q_gain))
    k = rope(rms_norm(da_k.reshape(b, s, 2 * DA_HEADS, DA_QK_DIM), qk_k_gain))
    q = (q * (DA_QK_DIM ** -0.5)).reshape(b, s, DA_HEADS, 2, DA_QK_DIM)
    k = k.reshape(b, s, DA_HEADS, 2, DA_QK_DIM)
    lam = (jnp.exp(jnp.sum(lambda_q1.astype(jnp.float32) * lambda_k1.astype(jnp.float32)))
           - jnp.exp(jnp.sum(lambda_q2.astype(jnp.float32) * lambda_k2.astype(jnp.float32)))
           + lambda_init)
    va = da_v.reshape(b, s, DA_HEADS, DA_V_DIM)
    oa = differential_attention(q[:, :, :, 0, :], q[:, :, :, 1, :],
                                k[:, :, :, 0, :], k[:, :, :, 1, :], va, lam)
    oa = rms_norm(oa.reshape(b, s, DA_HEADS, DA_V_DIM), subln_gain) * (1.0 - lambda_init)
    ua = oa.reshape(b, s, DA_WIDTH) * jax.nn.silu(da_z)

    qs = sb_q.reshape(b, s, SB_HEADS, SB_HEAD_DIM) * (SB_HEAD_DIM ** -0.5)
    ks = sb_k.reshape(b, s, SB_HEADS, SB_HEAD_DIM)
    vs = sb_v.reshape(b, s, SB_HEADS, SB_HEAD_DIM)
    ub = stick_breaking_attention(qs, ks, vs) * jax.nn.silu(sb_z)

    y = (jax.nn.sigmoid(gate_a) * jnp.einsum('bse,ed->bsd', ua, w_branch_a)
         + jax.nn.sigmoid(gate_b) * jnp.einsum('bse,ed->bsd', ub, w_branch_b))
    return x + jnp.einsum('bsd,de->bse', y, w_out)


def setup_inputs(seed: int = 0) -> dict:
    key = jax.random.key(seed)
    ks = jax.random.split(key, 13)
    f32 = jnp.float32
    x = jax.random.normal(ks[0], (BATCH, SEQ, D_MODEL), f32)
    norm_gain = 1.0 + 0.02 * jax.random.normal(ks[1], (DEPTH, D_MODEL), f32)
    w_in = jax.random.normal(ks[2], (DEPTH, D_MODEL, IN_COLS), f32) * D_MODEL ** -0.5
    qk_q_gain = 1.0 + 0.02 * jax.random.normal(ks[3], (DEPTH, DA_QK_DIM), f32)
    qk_k_gain = 1.0 + 0.02 * jax.random.normal(ks[4], (DEPTH, DA_QK_DIM), f32)
    lambda_q1 = LAMBDA_STD * jax.random.normal(ks[5], (DEPTH, DA_QK_DIM), f32)
    lambda_k1 = LAMBDA_STD * jax.random.normal(ks[6], (DEPTH, DA_QK_DIM), f32)
    lambda_q2 = LAMBDA_STD * jax.random.normal(ks[7], (DEPTH, DA_QK_DIM), f32)
    lambda_k2 = LAMBDA_STD * jax.random.normal(ks[8], (DEPTH, DA_QK_DIM), f32)
    subln_gain = 1.0 + 0.02 * jax.random.normal(ks[9], (DEPTH, DA_V_DIM), f32)
    w_branch_a = jax.random.normal(ks[10], (DEPTH, DA_WIDTH, D_MODEL), f32) * DA_WIDTH ** -0.5
    w_branch_b = jax.random.normal(ks[11], (DEPTH, SB_WIDTH, D_MODEL), f32) * SB_WIDTH ** -0.5
    w_out = jax.random.normal(ks[12], (DEPTH, D_MODEL, D_MODEL), f32) * D_MODEL ** -0.5
    return {"x": x, "norm_gain": norm_gain, "w_in": w_in,
            "qk_q_gain": qk_q_gain, "qk_k_gain": qk_k_gain,
            "lambda_q1": lambda_q1, "lambda_k1": lambda_k1,
            "lambda_q2": lambda_q2, "lambda_k2": lambda_k2,
            "subln_gain": subln_gain, "w_branch_a": w_branch_a,
            "w_branch_b": w_branch_b, "w_out": w_out}


def reference(x, norm_gain, w_in, qk_q_gain, qk_k_gain, lambda_q1, lambda_k1,
              lambda_q2, lambda_k2, subln_gain, w_branch_a, w_branch_b, w_out):
    for l in range(DEPTH):
        x = hybrid_layer(x, l, norm_gain[l], w_in[l], qk_q_gain[l], qk_k_gain[l],
                         lambda_q1[l], lambda_k1[l], lambda_q2[l], lambda_k2[l], subln_gain[l],
                         w_branch_a[l], w_branch_b[l], w_out[l])
    return x
```

```python
import math
import numpy as np
import ml_dtypes
import concourse.bass as bass
import concourse.mybir as mybir
from concourse.bass_utils import run_bass_kernel_spmd

F32 = mybir.dt.float32
BF16 = mybir.dt.bfloat16
AF = mybir.ActivationFunctionType
ALU = mybir.AluOpType
AX = mybir.AxisListType

NCORES = 8
D = 2048
SEQ = 8192
DEPTH = 4
KC = 16
TPC = SEQ // NCORES
NEG = -30000.0
RMS_EPS = 1e-6

EP = 12000
EPD = 1000


class Op:
    __slots__ = ("eng", "fn", "deps", "is_dma", "sem", "semval", "signal", "count", "idx")


class Sched:
    ENG = ("pe", "act", "dve", "pool", "sp")

    def __init__(self, nc):
        self.nc = nc
        self.q = {e: [] for e in self.ENG}
        self.last_w = {}
        self.readers = {}
        self.dma_cnt = {}
        self.last_dma = {}
        self.n = 0

    @staticmethod
    def _src(op):
        return ("d", op.sem) if op.is_dma else ("e", op.eng)

    def add(self, eng, fn, reads=(), writes=(), dma=None):
        op = Op()
        op.eng = eng
        op.fn = fn
        op.is_dma = dma is not None
        op.signal = False
        op.count = 0
        op.idx = self.n
        self.n += 1
        deps = {}

        def dep(o):
            s = self._src(o)
            cur = deps.get(s)
            if cur is None or o.idx > cur.idx:
                deps[s] = o

        for k in reads:
            w = self.last_w.get(k)
            if w is not None:
                dep(w)
        for k in writes:
            w = self.last_w.get(k)
            if w is not None:
                dep(w)
            for r in self.readers.get(k, {}).values():
                dep(r)
        op.deps = list(deps.values())
        if dma is not None:
            c = self.dma_cnt.get(dma, 0) + 1
            self.dma_cnt[dma] = c
            op.sem = dma
            op.semval = c
        else:
            op.sem = None
            op.semval = 0
        for k in reads:
            self.readers.setdefault(k, {})[self._src(op)] = op
        for k in writes:
            self.last_w[k] = op
            self.readers[k] = {}
        self.q[eng].append(op)
        if dma is not None:
            self.last_dma[dma] = op
        return op

    def emit(self):
        nc = self.nc
        for e in self.ENG:
            for op in self.q[e]:
                for d in op.deps:
                    if d.is_dma:
                        continue
                    if d.eng == "pe" and op.eng == "pe":
                        continue
                    d.signal = True
        nsig = {}
        for e in self.ENG:
            c = 0
            for op in self.q[e]:
                if op.signal and not op.is_dma:
                    c += 1
                    op.count = c
            nsig[e] = c
        self.esem = {}
        for e in self.ENG:
            for ep in range((nsig[e] + EP - 1) // EP):
                self.esem[(e, ep)] = nc.alloc_semaphore(f"s_{e}_{ep}")
        self.dsem = {}
        for name, c in self.dma_cnt.items():
            for ep in range((c + EPD - 1) // EPD):
                self.dsem[(name, ep)] = nc.alloc_semaphore(f"d_{name}_{ep}")
        allsems = list(self.esem.values()) + list(self.dsem.values())
        for sh in allsems:
            nc.gpsimd.sem_clear(sh)
        nc.all_engine_barrier()
        with nc.Block() as block:
            @block.tensor
            def _(eng):
                self._emit_engine("pe", eng)

            @block.scalar
            def _(eng):
                self._emit_engine("act", eng)

            @block.vector
            def _(eng):
                self._emit_engine("dve", eng)

            @block.gpsimd
            def _(eng):
                self._emit_engine("pool", eng)

            @block.sync
            def _(eng):
                self._emit_engine("sp", eng)
        nc.all_engine_barrier()
        for sh in allsems:
            nc.gpsimd.sem_clear(sh)
        nc.all_engine_barrier()

    def _emit_engine(self, e, eng):
        waited = {}
        for op in self.q[e]:
            for d in op.deps:
                if d.is_dma:
                    key = ("d", d.sem)
                    val = d.semval
                    if waited.get(key, 0) >= val:
                        continue
                    waited[key] = val
                    ep = (val - 1) // EPD
                    eng.wait_ge(self.dsem[(d.sem, ep)], 16 * (val - ep * EPD))
                else:
                    if d.eng == "pe" and e == "pe":
                        continue
                    key = ("e", d.eng)
                    val = d.count
                    if waited.get(key, 0) >= val:
                        continue
                    waited[key] = val
                    ep = (val - 1) // EP
                    eng.wait_ge(self.esem[(d.eng, ep)], val - ep * EP)
            ins = op.fn(eng)
            if op.is_dma:
                ep = (op.semval - 1) // EPD
                ins.then_inc(self.dsem[(op.sem, ep)], 16)
            elif op.signal:
                ep = (op.count - 1) // EP
                ins.then_inc(self.esem[(e, ep)], 1)


class B:
    def __init__(self, nc):
        self.nc = nc
        self.S = Sched(nc)

    def sb(self, name, shape, dt):
        return self.nc.alloc_sbuf_tensor(name, list(shape), dt)

    def ps(self, name, shape, dt=F32):
        return self.nc.alloc_psum_tensor(name, list(shape), dt)

    def load(self, out, in_, r=(), w=(), sem=None, q="sp"):
        return self.S.add(q, lambda e: e.dma_start(out=out, in_=in_), r, w, dma=sem)

    def store(self, out, in_, r=(), w=(), sem=None, q="pool"):
        return self.S.add(q, lambda e: e.dma_start(out=out, in_=in_), r, w, dma=sem)

    def mm(self, out, lhsT, rhs, start, stop, r=(), w=()):
        return self.S.add("pe", lambda e: e.matmul(out, lhsT=lhsT, rhs=rhs, start=start, stop=stop), r, w)

    def tr(self, out, in_, ident, r=(), w=()):
        return self.S.add("pe", lambda e: e.transpose(out=out, in_=in_, identity=ident), r, w)

    def act(self, out, in_, func, r=(), w=(), **kw):
        return self.S.add("act", lambda e: e.activation(out=out, in_=in_, func=func, **kw), r, w)

    def copy(self, eng, out, in_, r=(), w=()):
        if eng == "act":
            return self.S.add("act", lambda e: e.copy(out=out, in_=in_), r, w)
        return self.S.add(eng, lambda e: e.tensor_copy(out=out, in_=in_), r, w)

    def tt(self, eng, out, in0, in1, op, r=(), w=()):
        return self.S.add(eng, lambda e: e.tensor_tensor(out=out, in0=in0, in1=in1, op=op), r, w)

    def ts(self, eng, out, in0, s1, s2, op0, op1=None, r=(), w=()):
        if op1 is None:
            return self.S.add(eng, lambda e: e.tensor_scalar(out=out, in0=in0, scalar1=s1, scalar2=None, op0=op0), r, w)
        return self.S.add(eng, lambda e: e.tensor_scalar(out=out, in0=in0, scalar1=s1, scalar2=s2, op0=op0, op1=op1), r, w)

    def stt(self, eng, out, in0, scalar, in1, op0, op1, r=(), w=()):
        return self.S.add(eng, lambda e: e.scalar_tensor_tensor(out=out, in0=in0, scalar=scalar, in1=in1, op0=op0, op1=op1), r, w)

    def red(self, out, in_, r=(), w=()):
        return self.S.add("dve", lambda e: e.reduce_sum(out=out, in_=in_, axis=AX.X), r, w)

    def recip(self, out, in_, r=(), w=()):
        return self.S.add("dve", lambda e: e.reciprocal(out=out, in_=in_), r, w)

    def memset(self, eng, ap, val, r=(), w=()):
        return self.S.add(eng, lambda e: e.memset(ap, val), r, w)

    def asel(self, out, in_, pattern, cmp, fill, base, cm, r=(), w=()):
        return self.S.add("pool", lambda e: e.affine_select(out=out, in_=in_, pattern=pattern, compare_op=cmp,
                                                           fill=fill, base=base, channel_multiplier=cm), r, w)

    def rstd(self, out, ss, scale, tag):
        self.ts("dve", out, ss, scale, RMS_EPS, ALU.mult, ALU.add, r=[tag + "_ss"], w=[tag + "_r"])
        self.act(out, out, AF.Sqrt, r=[tag + "_r"], w=[tag + "_r"])
        self.recip(out, out, r=[tag + "_r"], w=[tag + "_r"])

    def finish(self):
        for q in ("sp", "pool"):
            op = self.S.add(q, lambda e: e.nop(), (), ())
            op.deps = list(self.S.last_dma.values())
        self.S.emit()


def make_ident(b, name="ident"):
    idf = b.sb(name + "_f", [128, 128], F32)
    idb = b.sb(name, [128, 128], BF16)
    b.memset("pool", idf[:], 0.0, w=[name + "_f"])
    b.asel(idf[:], idf[:], [[-1, 128]], ALU.not_equal, 1.0, 0, 1, r=[name + "_f"], w=[name + "_f"])
    b.copy("dve", idb[:], idf[:], r=[name + "_f"], w=[name])
    return idb


NSMALL = 6 * 128 + 256 + 2
WDA = 1024
WSB = 1536
WPC = 64


def build_phase_ab(nchunks=16):
    nc = bass.Bass("TRN2", target_bir_lowering=False)
    b = B(nc)
    S = b.S
    ntok = nchunks * 512
    hT = nc.dram_tensor("hT", [D, ntok], BF16, kind="ExternalInput").ap()
    w = nc.dram_tensor("w", [D, WDA + WSB], F32, kind="ExternalInput").ap()
    cs = nc.dram_tensor("cs", [ntok, 128], F32, kind="ExternalInput").ap()
    small = nc.dram_tensor("small", [128, NSMALL], F32, kind="ExternalInput").ap()
    ua = nc.dram_tensor("ua", [ntok, 256], BF16, kind="ExternalOutput").ap()
    ubT = nc.dram_tensor("ubT", [256, ntok], BF16, kind="ExternalOutput").ap()
    sgT = nc.dram_tensor("sgT", [512, ntok], BF16, kind="ExternalOutput").ap()

    hTv = hT.rearrange("(kc p) t -> p kc t", p=128)
    wv = w.rearrange("(kc p) n -> p kc n", p=128)
    csv = cs.rearrange("(n p) c -> p n c", p=128)

    WB = b.sb("WB", [128, KC, WSB], BF16)
    stg = [b.sb(f"stg{i}", [128, KC, WPC], F32) for i in range(2)]
    KT = b.sb("KT", [128, 2, ntok], BF16)
    V = b.sb("V", [128, ntok // 128, 257], BF16)
    HT = [b.sb(f"HT{i}", [128, KC, 512], BF16) for i in range(2)]
    sm = b.sb("sm", [128, NSMALL], F32)
    gain4 = b.sb("gain4", [128, 4, 128], F32)
    lam = b.sb("lam", [128, 1], F32)
    ltmp = b.sb("ltmp", [128, 128], F32)
    ld = b.sb("ld", [128, 2], F32)
    mtmp = b.sb("mtmp", [128, 512], F32)
    Dm = b.sb("Dm", [128, 2, 256], BF16)
    Mm = b.sb("Mm", [128, 4, 512], BF16)
    ntri = b.sb("ntri", [128, 128], BF16)
    nones = b.sb("nones", [128, 128], BF16)
    CS = b.sb("CS", [128, 4, 128], F32)
    T2 = [b.sb(f"T2_{i}", [128, 512], F32) for i in range(2)]
    RA = [b.sb(f"RA{i}", [128, 4, 64], F32) for i in range(4)]
    qkr = b.sb("qkr", [128, 4, 128], BF16)
    QT = [b.sb(f"QT{i}", [128, 2, 512], BF16) for i in range(2)]
    ZS = [b.sb(f"ZS{i}", [128, 1024], BF16) for i in range(2)]
    PW = [b.sb(f"PW{i}", [128, 512], BF16) for i in range(3)]
    SP = [b.sb(f"SP{i}", [128, 512], BF16) for i in range(3)]
    RACC = b.sb("RACC", [128, 512], BF16)
    st4 = b.sb("st4", [128, 8], F32)
    stq = b.sb("stq", [128, 8], F32)
    fo_t = b.sb("fo_t", [128, 256], F32)
    fo_oa = b.sb("fo_oa", [128, 256], F32)
    fo_j = b.sb("fo_j", [128, 256], F32)
    fo_st = b.sb("fo_st", [128, 8], F32)
    UAo = [b.sb(f"UAo{i}", [128, 256], BF16) for i in range(2)]
    SGo = [b.sb(f"SGo{i}", [128, 512], BF16) for i in range(3)]
    UBo = [b.sb(f"UBo{i}", [128, 512], BF16) for i in range(2)]

    PB = [b.ps(f"PB{i}", [128, 512]) for i in range(8)]
    BK = [("PB", i) for i in range(8)]

    ident = make_ident(b)

    b.load(sm[:], small, w=["sm"], sem="sm")
    inv = 128.0 ** -0.5
    for j in range(2):
        b.ts("dve", gain4[:, j, :], sm[:, 0:128], inv, None, ALU.mult, r=["sm"], w=[("g4", j)])
        b.copy("dve", gain4[:, 2 + j, :], sm[:, 128:256], r=["sm"], w=[("g4", 2 + j)])
    G4 = [("g4", j) for j in range(4)]
    for j in range(2):
        b.tt("dve", ltmp[:], sm[:, 256 + 256 * j:384 + 256 * j], sm[:, 384 + 256 * j:512 + 256 * j], ALU.mult,
             r=["sm"], w=["ltmp"])
        b.red(ld[:, j:j + 1], ltmp[:], r=["ltmp"], w=["ld"])
    b.act(ld[:], ld[:], AF.Exp, r=["ld"], w=["ld"])
    b.tt("dve", lam[:], ld[:, 0:1], ld[:, 1:2], ALU.subtract, r=["ld"], w=["lam"])
    b.tt("dve", lam[:], lam[:], sm[:, NSMALL - 2:NSMALL - 1], ALU.add, r=["lam", "sm"], w=["lam"])
    c1m = sm[:, NSMALL - 1:NSMALL]
    sgain = sm[:, 768:1024]
    for j in range(2):
        b.memset("pool", mtmp[:, 0:256], 0.0, w=["mtmp"])
        b.asel(mtmp[:, 0:256], mtmp[:, 0:256], [[1, 256]], ALU.is_ge, NEG, -128 * j, -1, r=["mtmp"], w=["mtmp"])
        b.copy("dve", Dm[:, j, :], mtmp[:, 0:256], r=["mtmp"], w=["Dm"])
    for j in range(4):
        b.memset("pool", mtmp[:], 0.0, w=["mtmp"])
        b.asel(mtmp[:], mtmp[:], [[1, 512]], ALU.is_gt, NEG, -128 * j, -1, r=["mtmp"], w=["mtmp"])
        b.copy("dve", Mm[:, j, :], mtmp[:], r=["mtmp"], w=["Mm"])
    b.memset("pool", mtmp[:, 0:128], -1.0, w=["mtmp"])
    b.copy("dve", nones[:], mtmp[:, 0:128], r=["mtmp"], w=["nones"])
    b.asel(mtmp[:, 0:128], mtmp[:, 0:128], [[-1, 128]], ALU.is_ge, 0.0, 0, 1, r=["mtmp"], w=["mtmp"])
    b.copy("dve", ntri[:], mtmp[:, 0:128], r=["mtmp"], w=["ntri"])
    b.memset("pool", V[:, :, 256:257], 1.0, w=["Vones"])

    stg_n = [0]

    def load_weights(c0, ncols, wkeys_prev):
        for p in range(ncols // WPC):
            i = stg_n[0] % 2
            stg_n[0] += 1
            b.load(stg[i][:], wv[:, :, c0 + p * WPC:c0 + (p + 1) * WPC], w=[("stg", i)], sem=f"stg{i}")
            eng = "dve" if p % 2 == 0 else "pool"
            b.copy(eng, WB[:, :, p * WPC:(p + 1) * WPC], stg[i][:], r=[("stg", i)],
                   w=[("WB", p)] + (wkeys_prev if p < 2 else []))
        return [("WB", p) for p in range(ncols // WPC)]

    def wkeys(c0, c1):
        return [("WB", p) for p in range(c0 // WPC, (c1 - 1) // WPC + 1)]

    load_weights(0, WDA, [])
    for ci in range(nchunks):
        hb = HT[ci % 2]
        hk = ("HT", ci % 2)
        b.load(hb[:], hTv[:, :, ci * 512:(ci + 1) * 512], w=[hk], sem=f"HT{ci % 2}")
        b.load(CS[:], csv[:, ci * 4:(ci + 1) * 4, :], w=["CS"], sem="CS")
        qt = QT[ci % 2]
        qk_ = ("QT", ci % 2)
        zs = ZS[ci % 2]
        zk = ("ZS", ci % 2)
        zs3 = zs[:].rearrange("p (a c) -> p a c", a=4)
        for tb in range(4):
            blk = ci * 4 + tb
            for kc in range(KC):
                b.mm(PB[0][:], hb[:, kc, tb * 128:(tb + 1) * 128], WB[:, kc, 0:512], kc == 0, kc == KC - 1,
                     r=[hk] + wkeys(0, 512), w=[BK[0]])
            b.act(T2[0][:], PB[0][:], AF.Square, r=[BK[0]], w=["T2_0"])
            b.red(st4[:, 0:4], T2[0][:].rearrange("p (a c) -> p a c", a=4), r=["T2_0"], w=["q_ss"])
            b.rstd(st4[:, 4:8], st4[:, 0:4], 1.0 / 128, "q")
            t1 = T2[1][:].rearrange("p (a c) -> p a c", a=4)
            b.tt("dve", t1, PB[0][:].rearrange("p (a c) -> p a c", a=4),
                 st4[:, 4:8].unsqueeze(2).broadcast_to([128, 4, 128]), ALU.mult, r=[BK[0], "q_r"], w=["T2_1"])
            b.tt("dve", t1, t1, gain4[:], ALU.mult, r=["T2_1"] + G4, w=["T2_1"])
            t4 = T2[1][:].rearrange("p (a h c) -> p a h c", a=4, h=2)
            x1 = t4[:, :, 0, :]
            x2 = t4[:, :, 1, :]
            cosb = CS[:, tb, 0:64].unsqueeze(1).broadcast_to([128, 4, 64])
            sinb = CS[:, tb, 64:128].unsqueeze(1).broadcast_to([128, 4, 64])
            b.tt("dve", RA[0][:], x1, cosb, ALU.mult, r=["T2_1", "CS"], w=["RA0"])
            b.tt("pool", RA[1][:], x2, sinb, ALU.mult, r=["T2_1", "CS"], w=["RA1"])
            b.tt("dve", RA[2][:], x2, cosb, ALU.mult, r=["T2_1", "CS"], w=["RA2"])
            b.tt("pool", RA[3][:], x1, sinb, ALU.mult, r=["T2_1", "CS"], w=["RA3"])
            q4 = qkr[:].rearrange("p a (h c) -> p a h c", h=2)
            b.tt("dve", q4[:, :, 0, :], RA[0][:], RA[1][:], ALU.subtract, r=["RA0", "RA1"], w=["qkr0"])
            b.tt("dve", q4[:, :, 1, :], RA[2][:], RA[3][:], ALU.add, r=["RA2", "RA3"], w=["qkr1"])
            pT = PB[1][:].bitcast(BF16)[:, 0:512].rearrange("p (a c) -> p a c", a=4)
            for a in range(4):
                b.tr(pT[:, a, :], qkr[:, a, :], ident[:], r=["qkr0", "qkr1", "ident"], w=[BK[1]])
            b.copy("act", qt[:, :, tb * 128:(tb + 1) * 128], pT[:, 0:2, :], r=[BK[1]], w=[qk_])
            b.copy("act", KT[:, :, blk * 128:(blk + 1) * 128], pT[:, 2:4, :], r=[BK[1]],
                   w=[("KT", blk)])
            for kc in range(KC):
                b.mm(PB[0][:], hb[:, kc, tb * 128:(tb + 1) * 128], WB[:, kc, 512:1024], kc == 0, kc == KC - 1,
                     r=[hk] + wkeys(512, 1024), w=[BK[0]])
            b.copy("act", V[:, blk, 0:256], PB[0][:, 0:256], r=[BK[0]], w=[("V", blk)])
            b.act(zs3[:, tb, :], PB[0][:, 256:512], AF.Silu, r=[BK[0]], w=[zk])
        for s in range(2):
            g = 2 * ci + s
            nkb = 2 * g + 2
            OB = [[PB[4 + 2 * m + bb] for bb in range(2)] for m in range(2)]
            for kb in range(nkb):
                sb_ = PB[2 + kb % 2]
                sk = BK[2 + kb % 2]
                diag = kb >= 2 * g
                for m in range(2):
                    b.mm(sb_[:, m * 256:(m + 1) * 256], KT[:, m, kb * 128:(kb + 1) * 128],
                         qt[:, m, s * 256:(s + 1) * 256], True, not diag, r=[("KT", kb), qk_], w=[sk])
                    if diag:
                        b.mm(sb_[:, m * 256:(m + 1) * 256], ident[:], Dm[:, kb - 2 * g, :], False, True,
                             r=["ident", "Dm"], w=[sk])
                pw = PW[kb % 3]
                pk = ("PW", kb % 3)
                b.act(pw[:], sb_[:], AF.Exp, r=[sk], w=[pk])
                for m in range(2):
                    for bb in range(2):
                        b.mm(OB[m][bb][:, 0:257], pw[:, m * 256 + bb * 128:m * 256 + (bb + 1) * 128], V[:, kb, :],
                             kb == 0, kb == nkb - 1, r=[pk, ("V", kb), "Vones"], w=[BK[4 + 2 * m + bb]])
            for bb in range(2):
                qb = 2 * g + bb
                tb = 2 * s + bb
                O1 = OB[0][bb]
                O2 = OB[1][bb]
                b.recip(fo_st[:, 0:1], O1[:, 256:257], r=[BK[4 + bb]], w=["fo_r1"])
                b.recip(fo_st[:, 1:2], O2[:, 256:257], r=[BK[6 + bb]], w=["fo_r2"])
                b.tt("dve", fo_st[:, 1:2], fo_st[:, 1:2], lam[:], ALU.mult, r=["fo_r2", "lam"], w=["fo_r2"])
                b.ts("dve", fo_t[:], O2[:, 0:256], fo_st[:, 1:2], None, ALU.mult, r=[BK[6 + bb], "fo_r2"],
                     w=["fo_t"])
                b.stt("dve", fo_oa[:], O1[:, 0:256], fo_st[:, 0:1], fo_t[:], ALU.mult, ALU.subtract,
                      r=[BK[4 + bb], "fo_r1", "fo_t"], w=["fo_oa"])
                b.act(fo_j[:], fo_oa[:], AF.Square, r=["fo_oa"], w=["fo_j", "s_ss"], accum_out=fo_st[:, 2:3])
                b.rstd(fo_st[:, 3:4], fo_st[:, 2:3], 1.0 / 256, "s")
                b.tt("dve", fo_st[:, 3:4], fo_st[:, 3:4], c1m, ALU.mult, r=["s_r", "sm"], w=["s_r"])
                b.stt("dve", fo_t[:], fo_oa[:], fo_st[:, 3:4], sgain, ALU.mult, ALU.mult, r=["fo_oa", "s_r", "sm"],
                      w=["fo_t"])
                uo = UAo[qb % 2]
                b.tt("dve", uo[:], fo_t[:], zs3[:, tb, :], ALU.mult, r=["fo_t", zk], w=[("UAo", qb % 2)])
                b.store(ua[qb * 128:(qb + 1) * 128, :], uo[:], r=[("UAo", qb % 2)], w=["ua_out"], sem=f"UAo{qb % 2}")

    load_weights(WDA, WSB, BK)
    OTB = [PB[6], PB[7]]
    for ci in range(nchunks):
        hb = HT[ci % 2]
        hk = ("HT", ci % 2)
        b.load(hb[:], hTv[:, :, ci * 512:(ci + 1) * 512], w=[hk], sem=f"HT{ci % 2}")
        qt = QT[ci % 2]
        qk_ = ("QT", ci % 2)
        zs = ZS[ci % 2]
        zk = ("ZS", ci % 2)
        zs2 = zs[:].rearrange("p (a c) -> p a c", a=2)
        for j in range(10):
            pb = PB[j % 2]
            pk = BK[j % 2]
            for kc in range(KC):
                b.mm(pb[:], WB[:, kc, j * 128:(j + 1) * 128], hb[:, kc, :], kc == 0, kc == KC - 1,
                     r=[hk] + wkeys(j * 128, (j + 1) * 128), w=[pk])
            if j < 2:
                b.ts("dve", qt[:, j, :], pb[:], inv, None, ALU.mult, r=[pk], w=[qk_])
            elif j < 4:
                b.copy("act", KT[:, j - 2, ci * 512:(ci + 1) * 512], pb[:], r=[pk],
                       w=[("KT", ci * 4 + t) for t in range(4)])
            elif j < 6:
                b.act(zs2[:, j - 4, :], pb[:], AF.Silu, r=[pk], w=[zk])
            else:
                so = SGo[j % 3]
                b.act(so[:], pb[:], AF.Sigmoid, r=[pk], w=[("SGo", j % 3)])
                b.store(sgT[(j - 6) * 128:(j - 5) * 128, ci * 512:(ci + 1) * 512], so[:], r=[("SGo", j % 3)],
                        w=["sg_out"], sem=f"SGo{j % 3}")
        for tb in range(4):
            blk = ci * 4 + tb
            pb = PB[tb % 2]
            pk = BK[tb % 2]
            for kc in range(KC):
                b.mm(pb[:, 0:256], hb[:, kc, tb * 128:(tb + 1) * 128], WB[:, kc, 1280:1536], kc == 0, kc == KC - 1,
                     r=[hk] + wkeys(1280, 1536), w=[pk])
            b.copy("act", V[:, blk, 0:256], pb[:, 0:256], r=[pk], w=[("V", blk)])
        nkb = 4 * ci + 4
        xi = 0
        for h in range(2):
            ot = OTB[h]
            ok = BK[6 + h]
            first = True
            for kb in range(nkb - 1, -1, -1):
                xb = PB[2 + xi % 4]
                xk = BK[2 + xi % 4]
                e_t = T2[xi % 2]
                ek = f"T2_{xi % 2}"
                sp = SP[xi % 3]
                spk = ("SP", xi % 3)
                pw = PW[xi % 3]
                pwk = ("PW", xi % 3)
                xi += 1
                jd = kb - 4 * ci
                b.mm(xb[:], KT[:, h, kb * 128:(kb + 1) * 128], qt[:, h, :], True, False, r=[("KT", kb), qk_], w=[xk])
                if jd >= 0:
                    b.mm(xb[:], ident[:], Mm[:, jd, :], False, False, r=["ident", "Mm"], w=[xk])
                b.act(e_t[:], xb[:], AF.Exp, r=[xk], w=[ek])
                b.act(sp[:], e_t[:], AF.Ln, r=[ek], w=[spk], bias=1.0)
                b.mm(xb[:], ntri[:], sp[:], False, first, r=["ntri", spk], w=[xk])
                if not first:
                    b.mm(xb[:], nones[:], RACC[:], False, True, r=["nones", "RACC"], w=[xk])
                b.act(pw[:], xb[:], AF.Exp, r=[xk], w=[pwk])
                b.mm(ot[:], V[:, kb, h * 128:(h + 1) * 128], pw[:], first, kb == 0, r=[("V", kb), pwk], w=[ok])
                if kb > 0:
                    if first:
                        b.copy("dve", RACC[:], sp[:], r=[spk], w=["RACC"])
                    else:
                        b.tt("dve", RACC[:], RACC[:], sp[:], ALU.add, r=[spk, "RACC"], w=["RACC"])
                first = False
            uo = UBo[h]
            b.tt("dve", uo[:], ot[:], zs2[:, h, :], ALU.mult, r=[ok, zk], w=[("UBo", h)])
            b.store(ubT[h * 128:(h + 1) * 128, ci * 512:(ci + 1) * 512], uo[:], r=[("UBo", h)], w=["ub_out"],
                    sem=f"UBo{h}")
    b.finish()
    return nc


def build_phase_c(merge=True, norm=True):
    nc = bass.Bass("TRN2", target_bir_lowering=False)
    b = B(nc)
    T = TPC
    NB = T // 128
    x = nc.dram_tensor("x", [T, D], F32, kind="ExternalInput").ap()
    ng = nc.dram_tensor("ng", [128, D], F32, kind="ExternalInput").ap()
    if merge:
        uaD = nc.dram_tensor("ua", [T, D], BF16, kind="ExternalInput").ap()
        ubD = nc.dram_tensor("ubT", [D, T], BF16, kind="ExternalInput").ap()
        sgaD = nc.dram_tensor("sgaT", [D, T], BF16, kind="ExternalInput").ap()
        sgbD = nc.dram_tensor("sgbT", [D, T], BF16, kind="ExternalInput").ap()
        waD = nc.dram_tensor("wa", [D, D], F32, kind="ExternalInput").ap()
        wbD = nc.dram_tensor("wb", [D, D], F32, kind="ExternalInput").ap()
        woD = nc.dram_tensor("wo", [D, D], F32, kind="ExternalInput").ap()
        xo = nc.dram_tensor("xo", [T, D], F32, kind="ExternalOutput").ap()
    if norm:
        hTo = nc.dram_tensor("hTo", [D, T], BF16, kind="ExternalOutput").ap()
        hTov = hTo.rearrange("(kc p) t -> p kc t", p=128)
    ident = make_ident(b)
    outs = []
    XB = [b.sb(f"XB{i}", [128, D], F32) for i in range(2)]
    if norm:
        NG = b.sb("NG", [128, D], F32)
        b.load(NG[:], ng, w=["NG"], sem="NG")
        junk = b.sb("junk", [128, D], BF16)
        nst = b.sb("nst", [128, 2], F32)
        HB = b.sb("HB", [128, D], BF16)
        HTO = [b.sb(f"HTO{i}", [128, KC, 512], BF16) for i in range(1)]
        PT = [b.ps(f"PT{i}", [128, 4, 128], BF16) for i in range(2)]
    if merge:
        BIG = b.sb("BIG", [128, 2, KC, T], BF16)
        YT = b.sb("YT", [128, KC, T], BF16)
        UAB = [b.sb(f"UAB{i}", [128, D], BF16) for i in range(1)]
        WS = [b.sb(f"WS{i}", [128, KC, 128], F32) for i in range(2)]
        WBF = [b.sb(f"WBF{i}", [128, KC, 128], BF16) for i in range(4)]
        SG = [b.sb(f"SG{i}", [128, 2, T], BF16) for i in range(2)]
        YA = [b.sb(f"YA{i}", [128, 512], F32) for i in range(2)]
        YB = [b.sb(f"YB{i}", [128, 512], F32) for i in range(2)]
        PA = [b.ps(f"PA{i}", [128, 512]) for i in range(4)]
        PTm = [b.ps(f"PTm{i}", [128, 4, 128], BF16) for i in range(2)]
        for kc in range(KC):
            b.load(BIG[:, 1, kc, :], ubD[kc * 128:(kc + 1) * 128, :], w=[("ubT", kc)], sem="ubT")
        for nb in range(NB):
            ub_ = UAB[0]
            b.load(ub_[:], uaD[nb * 128:(nb + 1) * 128, :], w=[("UAB", 0)], sem="UAB0")
            for q4 in range(4):
                pt = PTm[q4 % 2]
                for a in range(4):
                    kc = q4 * 4 + a
                    b.tr(pt[:, a, :], ub_[:, kc * 128:(kc + 1) * 128], ident[:], r=[("UAB", 0), "ident"],
                         w=[("PTm", q4 % 2, a)])
                b.copy("act" if q4 % 2 else "dve", BIG[:, 0, q4 * 4:(q4 + 1) * 4, nb * 128:(nb + 1) * 128], pt[:],
                       r=[("PTm", q4 % 2, a) for a in range(4)], w=[("uaT", nb)])
        UAT = [("uaT", nb) for nb in range(NB)]
        UBT = [("ubT", kc) for kc in range(KC)]
        for fo in range(KC):
            i2 = fo % 2
            for br, wD in enumerate((waD, wbD)):
                si = 2 * i2 + br
                b.load(WS[br][:], wD.rearrange("(kc p) n -> p kc n", p=128)[:, :, fo * 128:(fo + 1) * 128],
                       w=[("WS", br)], sem=f"WS{br}")
                b.copy("pool" if br else "dve", WBF[si][:], WS[br][:], r=[("WS", br)], w=[("WBF", si)])
            b.load(SG[i2][:, 0, :], sgaD[fo * 128:(fo + 1) * 128, :], w=[("SG", i2, 0)], sem=f"SGa{i2}")
            b.load(SG[i2][:, 1, :], sgbD[fo * 128:(fo + 1) * 128, :], w=[("SG", i2, 1)], sem=f"SGb{i2}")
            for th in range(T // 512):
                for br in range(2):
                    pa = PA[2 * (th % 2) + br]
                    pk = ("PA", 2 * (th % 2) + br)
                    si = 2 * i2 + br
                    for kc in range(KC):
                        b.mm(pa[:], WBF[si][:, kc, :], BIG[:, br, kc, th * 512:(th + 1) * 512], kc == 0, kc == KC - 1,
                             r=[("WBF", si)] + (UAT if br == 0 else UBT), w=[pk])
                ya = YA[th % 2]
                yk = ("YA", th % 2)
                b.tt("dve", ya[:], PA[2 * (th % 2)][:], SG[i2][:, 0, th * 512:(th + 1) * 512], ALU.mult,
                     r=[("PA", 2 * (th % 2)), ("SG", i2, 0)], w=[yk])
                yb = YB[th % 2]
                ybk = ("YB", th % 2)
                b.tt("dve", yb[:], PA[2 * (th % 2) + 1][:],
                     SG[i2][:, 1, th * 512:(th + 1) * 512], ALU.mult, r=[("PA", 2 * (th % 2) + 1), ("SG", i2, 1)],
                     w=[ybk])
                b.tt("pool", YT[:, fo, th * 512:(th + 1) * 512], yb[:], ya[:], ALU.add,
                     r=[ybk, yk], w=[("YT", fo, th)])
        YTK = [("YT", fo, th) for fo in range(KC) for th in range(T // 512)]
        WO = BIG[:].rearrange("p a k t -> p (a k t)").rearrange("p (k n) -> p k n", k=KC)
        for fo in range(KC):
            si = fo % 2
            b.load(WS[si][:], woD.rearrange("(kc p) n -> p kc n", p=128)[:, :, fo * 128:(fo + 1) * 128],
                   w=[("WS", si)], sem=f"WS{si}")
            b.copy("pool" if fo % 2 else "dve", WO[:, :, fo * 128:(fo + 1) * 128], WS[si][:], r=[("WS", si)],
                   w=[("WO", fo)] + UAT + UBT)
    for nb in range(NB):
        xb = XB[nb % 2]
        xk = ("XB", nb % 2)
        b.load(xb[:], x[nb * 128:(nb + 1) * 128, :], w=[xk], sem=f"XB{nb % 2}")
        if merge:
            for cg in range(4):
                pa = PA[cg]
                pk = ("PA", cg)
                for kc in range(KC):
                    b.mm(pa[:], YT[:, kc, nb * 128:(nb + 1) * 128], WO[:, kc, cg * 512:(cg + 1) * 512], kc == 0,
                         kc == KC - 1, r=YTK + [("WO", f) for f in range(cg * 4, cg * 4 + 4)], w=[pk])
                b.tt("dve", xb[:, cg * 512:(cg + 1) * 512], xb[:, cg * 512:(cg + 1) * 512], pa[:], ALU.add,
                     r=[xk, pk], w=[xk])
            b.store(xo[nb * 128:(nb + 1) * 128, :], xb[:], r=[xk], w=["xo_out"], sem=f"XO{nb % 2}")
            outs.append("xo_out")
        if norm:
            b.act(junk[:], xb[:], AF.Square, r=[xk], w=["junk", "n_ss"], accum_out=nst[:, 0:1])
            b.rstd(nst[:, 1:2], nst[:, 0:1], 1.0 / D, "n")
            b.stt("dve", HB[:], xb[:], nst[:, 1:2], NG[:], ALU.mult, ALU.mult, r=[xk, "n_r", "NG"], w=["HB"])
            ho = HTO[0]
            hok = ("HTO", 0)
            for q4 in range(4):
                pt = PT[q4 % 2]
                for a in range(4):
                    kc = q4 * 4 + a
                    b.tr(pt[:, a, :], HB[:, kc * 128:(kc + 1) * 128], ident[:], r=["HB", "ident"],
                         w=[("PT", q4 % 2, a)])
                b.copy("act" if q4 % 2 else "dve", ho[:, q4 * 4:(q4 + 1) * 4, (nb % 4) * 128:(nb % 4 + 1) * 128], pt[:],
                       r=[("PT", q4 % 2, a) for a in range(4)], w=[hok])
            if nb % 4 == 3:
                c4 = nb // 4
                b.store(hTov[:, :, c4 * 512:(c4 + 1) * 512], ho[:], r=[hok], w=["hT_out"], sem="HTO0")
                outs.append("hT_out")
    b.finish()
    return nc


_CACHE = {}


def _prog(name, fn):
    if name not in _CACHE:
        _CACHE[name] = fn()
    return _CACHE[name]


def _run(nc, maps):
    res = run_bass_kernel_spmd(nc, maps, core_ids=list(range(NCORES)))
    return res.results


def _rope_table():
    d = 128
    inv_freq = np.exp(-(np.arange(0, d, 2, dtype=np.float32) / d) * math.log(10000.0)).astype(np.float32)
    ang = np.arange(SEQ, dtype=np.float32)[:, None] * inv_freq[None, :]
    return np.concatenate([np.cos(ang), np.sin(ang)], axis=1).astype(np.float32)


def _w_cols(c):
    seg = 2048
    r = lambda s, a, n: np.arange(s * seg + a, s * seg + a + n)
    cols = [r(0, 256 * c, 256), r(1, 256 * c, 256), r(2, 256 * c, 256), r(3, 256 * c, 256),
            r(4, 256 * c, 256), r(5, 256 * c, 256), r(7, 256 * c, 256),
            r(8, 256 * c, 256), r(9, 256 * c, 256), r(6, 256 * c, 256)]
    return np.concatenate(cols)


def kernel(x, norm_gain, w_in, qk_q_gain, qk_k_gain, lambda_q1, lambda_k1, lambda_q2, lambda_k2,
           subln_gain, w_branch_a, w_branch_b, w_out):
    bf = ml_dtypes.bfloat16
    rep = lambda v: np.ascontiguousarray(np.broadcast_to(np.asarray(v, np.float32)[None, :], (128, v.shape[-1])))
    xs = [np.ascontiguousarray(x[0, c * TPC:(c + 1) * TPC, :]) for c in range(NCORES)]
    cs = _rope_table()
    pn = _prog("n", lambda: build_phase_c(merge=False, norm=True))
    pab = _prog("ab", build_phase_ab)
    pc = _prog("c", lambda: build_phase_c(merge=True, norm=True))
    pcl = _prog("cl", lambda: build_phase_c(merge=True, norm=False))
    res = _run(pn, [{"x": xs[c], "ng": rep(norm_gain[0])} for c in range(NCORES)])
    hT = np.concatenate([res[c]["hTo"] for c in range(NCORES)], axis=1)
    for l in range(DEPTH):
        li = 0.8 - 0.6 * math.exp(-0.3 * l)
        small = np.concatenate([rep(qk_q_gain[l]), rep(qk_k_gain[l]), rep(lambda_q1[l]), rep(lambda_k1[l]),
                                rep(lambda_q2[l]), rep(lambda_k2[l]), rep(subln_gain[l]),
                                np.full((128, 1), li, np.float32), np.full((128, 1), 1.0 - li, np.float32)], axis=1)
        maps = [{"hT": hT, "w": np.ascontiguousarray(w_in[l][:, _w_cols(c)]), "cs": cs, "small": small}
                for c in range(NCORES)]
        res = _run(pab, maps)
        ua = np.concatenate([res[c]["ua"] for c in range(NCORES)], axis=1)
        ubT = np.concatenate([res[c]["ubT"] for c in range(NCORES)], axis=0)
        sgaT = np.concatenate([res[c]["sgT"][0:256] for c in range(NCORES)], axis=0)
        sgbT = np.concatenate([res[c]["sgT"][256:512] for c in range(NCORES)], axis=0)
        last = l == DEPTH - 1
        maps = []
        for c in range(NCORES):
            t0, t1 = c * TPC, (c + 1) * TPC
            maps.append({"x": xs[c], "ng": rep(norm_gain[min(l + 1, DEPTH - 1)]),
                         "ua": np.ascontiguousarray(ua[t0:t1]), "ubT": np.ascontiguousarray(ubT[:, t0:t1]),
                         "sgaT": np.ascontiguousarray(sgaT[:, t0:t1]), "sgbT": np.ascontiguousarray(sgbT[:, t0:t1]),
                         "wa": w_branch_a[l], "wb": w_branch_b[l], "wo": w_out[l]})
        res = _run(pcl if last else pc, maps)
        xs = [res[c]["xo"] for c in range(NCORES)]
        if not last:
            hT = np.concatenate([res[c]["hTo"] for c in range(NCORES)], axis=1)
    return np.concatenate(xs, axis=0)[None].astype(np.float32)
```

```python
import math
import numpy as np
import ml_dtypes
import concourse.bass as bass
import concourse.mybir as mybir
from concourse.bass_utils import run_bass_kernel_spmd

F32 = mybir.dt.float32
BF16 = mybir.dt.bfloat16
AF = mybir.ActivationFunctionType
ALU = mybir.AluOpType
AX = mybir.AxisListType

NCORES = 8
D = 2048
SEQ = 8192
DEPTH = 4
KC = 16
TPC = SEQ // NCORES
NEG = -30000.0
RMS_EPS = 1e-6

EP = 12000
EPD = 1000


class Op:
    __slots__ = ("eng", "fn", "deps", "is_dma", "sem", "semval", "signal", "count", "idx")


class Sched:
    ENG = ("pe", "act", "dve", "pool", "sp")

    def __init__(self, nc):
        self.nc = nc
        self.q = {e: [] for e in self.ENG}
        self.last_w = {}
        self.readers = {}
        self.dma_cnt = {}
        self.last_dma = {}
        self.n = 0

    @staticmethod
    def _src(op):
        return ("d", op.sem) if op.is_dma else ("e", op.eng)

    def add(self, eng, fn, reads=(), writes=(), dma=None):
        op = Op()
        op.eng = eng
        op.fn = fn
        op.is_dma = dma is not None
        op.signal = False
        op.count = 0
        op.idx = self.n
        self.n += 1
        deps = {}

        def dep(o):
            s = self._src(o)
            cur = deps.get(s)
            if cur is None or o.idx > cur.idx:
                deps[s] = o

        for k in reads:
            w = self.last_w.get(k)
            if w is not None:
                dep(w)
        for k in writes:
            w = self.last_w.get(k)
            if w is not None:
                dep(w)
            for r in self.readers.get(k, {}).values():
                dep(r)
        op.deps = list(deps.values())
        if dma is not None:
            c = self.dma_cnt.get(dma, 0) + 1
            self.dma_cnt[dma] = c
            op.sem = dma
            op.semval = c
        else:
            op.sem = None
            op.semval = 0
        for k in reads:
            self.readers.setdefault(k, {})[self._src(op)] = op
        for k in writes:
            self.last_w[k] = op
            self.readers[k] = {}
        self.q[eng].append(op)
        if dma is not None:
            self.last_dma[dma] = op
        return op

    def emit(self):
        nc = self.nc
        for e in self.ENG:
            for op in self.q[e]:
                for d in op.deps:
                    if d.is_dma:
                        continue
                    if d.eng == "pe" and op.eng == "pe":
                        continue
                    d.signal = True
        nsig = {}
        for e in self.ENG:
            c = 0
            for op in self.q[e]:
                if op.signal and not op.is_dma:
                    c += 1
                    op.count = c
            nsig[e] = c
        self.esem = {}
        for e in self.ENG:
            for ep in range((nsig[e] + EP - 1) // EP):
                self.esem[(e, ep)] = nc.alloc_semaphore(f"s_{e}_{ep}")
        self.dsem = {}
        for name, c in self.dma_cnt.items():
            for ep in range((c + EPD - 1) // EPD):
                self.dsem[(name, ep)] = nc.alloc_semaphore(f"d_{name}_{ep}")
        allsems = list(self.esem.values()) + list(self.dsem.values())
        for sh in allsems:
            nc.gpsimd.sem_clear(sh)
        nc.all_engine_barrier()
        with nc.Block() as block:
            @block.tensor
            def _(eng):
                self._emit_engine("pe", eng)

            @block.scalar
            def _(eng):
                self._emit_engine("act", eng)

            @block.vector
            def _(eng):
                self._emit_engine("dve", eng)

            @block.gpsimd
            def _(eng):
                self._emit_engine("pool", eng)

            @block.sync
            def _(eng):
                self._emit_engine("sp", eng)
        nc.all_engine_barrier()
        for sh in allsems:
            nc.gpsimd.sem_clear(sh)
        nc.all_engine_barrier()

    def _emit_engine(self, e, eng):
        waited = {}
        for op in self.q[e]:
            for d in op.deps:
                if d.is_dma:
                    key = ("d", d.sem)
                    val = d.semval
                    if waited.get(key, 0) >= val:
                        continue
                    waited[key] = val
                    ep = (val - 1) // EPD
                    eng.wait_ge(self.dsem[(d.sem, ep)], 16 * (val - ep * EPD))
                else:
                    if d.eng == "pe" and e == "pe":
                        continue
                    key = ("e", d.eng)
                    val = d.count
                    if waited.get(key, 0) >= val:
                        continue
                    waited[key] = val
                    ep = (val - 1) // EP
                    eng.wait_ge(self.esem[(d.eng, ep)], val - ep * EP)
            ins = op.fn(eng)
            if op.is_dma:
                ep = (op.semval - 1) // EPD
                ins.then_inc(self.dsem[(op.sem, ep)], 16)
            elif op.signal:
                ep = (op.count - 1) // EP
                ins.then_inc(self.esem[(e, ep)], 1)


class B:
    def __init__(self, nc):
        self.nc = nc
        self.S = Sched(nc)

    def sb(self, name, shape, dt):
        return self.nc.alloc_sbuf_tensor(name, list(shape), dt)

    def ps(self, name, shape, dt=F32):
        return self.nc.alloc_psum_tensor(name, list(shape), dt)

    def load(self, out, in_, r=(), w=(), sem=None, q="sp"):
        return self.S.add(q, lambda e: e.dma_start(out=out, in_=in_), r, w, dma=sem)

    def store(self, out, in_, r=(), w=(), sem=None, q="pool"):
        return self.S.add(q, lambda e: e.dma_start(out=out, in_=in_), r, w, dma=sem)

    def mm(self, out, lhsT, rhs, start, stop, r=(), w=()):
        return self.S.add("pe", lambda e: e.matmul(out, lhsT=lhsT, rhs=rhs, start=start, stop=stop), r, w)

    def tr(self, out, in_, ident, r=(), w=()):
        return self.S.add("pe", lambda e: e.transpose(out=out, in_=in_, identity=ident), r, w)

    def act(self, out, in_, func, r=(), w=(), **kw):
        return self.S.add("act", lambda e: e.activation(out=out, in_=in_, func=func, **kw), r, w)

    def copy(self, eng, out, in_, r=(), w=()):
        if eng == "act":
            return self.S.add("act", lambda e: e.copy(out=out, in_=in_), r, w)
        return self.S.add(eng, lambda e: e.tensor_copy(out=out, in_=in_), r, w)

    def tt(self, eng, out, in0, in1, op, r=(), w=()):
        return self.S.add(eng, lambda e: e.tensor_tensor(out=out, in0=in0, in1=in1, op=op), r, w)

    def ts(self, eng, out, in0, s1, s2, op0, op1=None, r=(), w=()):
        if op1 is None:
            return self.S.add(eng, lambda e: e.tensor_scalar(out=out, in0=in0, scalar1=s1, scalar2=None, op0=op0), r, w)
        return self.S.add(eng, lambda e: e.tensor_scalar(out=out, in0=in0, scalar1=s1, scalar2=s2, op0=op0, op1=op1), r, w)

    def stt(self, eng, out, in0, scalar, in1, op0, op1, r=(), w=()):
        return self.S.add(eng, lambda e: e.scalar_tensor_tensor(out=out, in0=in0, scalar=scalar, in1=in1, op0=op0, op1=op1), r, w)

    def red(self, out, in_, r=(), w=()):
        return self.S.add("dve", lambda e: e.reduce_sum(out=out, in_=in_, axis=AX.X), r, w)

    def recip(self, out, in_, r=(), w=()):
        return self.S.add("dve", lambda e: e.reciprocal(out=out, in_=in_), r, w)

    def memset(self, eng, ap, val, r=(), w=()):
        return self.S.add(eng, lambda e: e.memset(ap, val), r, w)

    def asel(self, out, in_, pattern, cmp, fill, base, cm, r=(), w=()):
        return self.S.add("pool", lambda e: e.affine_select(out=out, in_=in_, pattern=pattern, compare_op=cmp,
                                                           fill=fill, base=base, channel_multiplier=cm), r, w)

    def rstd(self, out, ss, scale, tag):
        self.ts("dve", out, ss, scale, RMS_EPS, ALU.mult, ALU.add, r=[tag + "_ss"], w=[tag + "_r"])
        self.act(out, out, AF.Sqrt, r=[tag + "_r"], w=[tag + "_r"])
        self.recip(out, out, r=[tag + "_r"], w=[tag + "_r"])

    def finish(self):
        for q in ("sp", "pool"):
            op = self.S.add(q, lambda e: e.nop(), (), ())
            op.deps = list(self.S.last_dma.values())
        self.S.emit()


def make_ident(b, name="ident"):
    idf = b.sb(name + "_f", [128, 128], F32)
    idb = b.sb(name, [128, 128], BF16)
    b.memset("pool", idf[:], 0.0, w=[name + "_f"])
    b.asel(idf[:], idf[:], [[-1, 128]], ALU.not_equal, 1.0, 0, 1, r=[name + "_f"], w=[name + "_f"])
    b.copy("dve", idb[:], idf[:], r=[name + "_f"], w=[name])
    return idb


NSMALL = 6 * 128 + 256 + 2
WDA = 1024
WSB = 1536
WPC = 64


def build_phase_ab(nchunks=16):
    nc = bass.Bass("TRN2", target_bir_lowering=False)
    b = B(nc)
    S = b.S
    ntok = nchunks * 512
    hT = nc.dram_tensor("hT", [D, ntok], BF16, kind="ExternalInput").ap()
    w = nc.dram_tensor("w", [D, WDA + WSB], F32, kind="ExternalInput").ap()
    cs = nc.dram_tensor("cs", [ntok, 128], F32, kind="ExternalInput").ap()
    small = nc.dram_tensor("small", [128, NSMALL], F32, kind="ExternalInput").ap()
    ua = nc.dram_tensor("ua", [ntok, 256], BF16, kind="ExternalOutput").ap()
    ubT = nc.dram_tensor("ubT", [256, ntok], BF16, kind="ExternalOutput").ap()
    sgT = nc.dram_tensor("sgT", [512, ntok], BF16, kind="ExternalOutput").ap()

    hTv = hT.rearrange("(kc p) t -> p kc t", p=128)
    wv = w.rearrange("(kc p) n -> p kc n", p=128)
    csv = cs.rearrange("(n p) c -> p n c", p=128)

    WB = b.sb("WB", [128, KC, WSB], BF16)
    stg = [b.sb(f"stg{i}", [128, KC, WPC], F32) for i in range(2)]
    KT = b.sb("KT", [128, 2, ntok], BF16)
    V = b.sb("V", [128, ntok // 128, 257], BF16)
    HT = [b.sb(f"HT{i}", [128, KC, 512], BF16) for i in range(2)]
    sm = b.sb("sm", [128, NSMALL], F32)
    gain4 = b.sb("gain4", [128, 4, 128], F32)
    lam = b.sb("lam", [128, 1], F32)
    ltmp = b.sb("ltmp", [128, 128], F32)
    ld = b.sb("ld", [128, 2], F32)
    mtmp = b.sb("mtmp", [128, 512], F32)
    Dm = b.sb("Dm", [128, 2, 256], BF16)
    Mm = b.sb("Mm", [128, 4, 512], BF16)
    ntri = b.sb("ntri", [128, 128], BF16)
    nones = b.sb("nones", [128, 128], BF16)
    CS = b.sb("CS", [128, 4, 128], F32)
    T2 = [b.sb(f"T2_{i}", [128, 512], F32) for i in range(2)]
    RA = [b.sb(f"RA{i}", [128, 4, 64], F32) for i in range(4)]
    qkr = b.sb("qkr", [128, 4, 128], BF16)
    QT = [b.sb(f"QT{i}", [128, 2, 512], BF16) for i in range(2)]
    ZS = [b.sb(f"ZS{i}", [128, 1024], BF16) for i in range(2)]
    PW = [b.sb(f"PW{i}", [128, 512], BF16) for i in range(3)]
    SP = [b.sb(f"SP{i}", [128, 512], BF16) for i in range(3)]
    RACC = b.sb("RACC", [128, 512], BF16)
    st4 = b.sb("st4", [128, 8], F32)
    stq = b.sb("stq", [128, 8], F32)
    fo_t = b.sb("fo_t", [128, 256], F32)
    fo_oa = b.sb("fo_oa", [128, 256], F32)
    fo_j = b.sb("fo_j", [128, 256], F32)
    fo_st = b.sb("fo_st", [128, 8], F32)
    UAo = [b.sb(f"UAo{i}", [128, 256], BF16) for i in range(2)]
    SGo = [b.sb(f"SGo{i}", [128, 512], BF16) for i in range(3)]
    UBo = [b.sb(f"UBo{i}", [128, 512], BF16) for i in range(2)]

    PB = [b.ps(f"PB{i}", [128, 512]) for i in range(8)]
    BK = [("PB", i) for i in range(8)]

    ident = make_ident(b)

    b.load(sm[:], small, w=["sm"], sem="sm")
    inv = 128.0 ** -0.5
    for j in range(2):
        b.ts("dve", gain4[:, j, :], sm[:, 0:128], inv, None, ALU.mult, r=["sm"], w=[("g4", j)])
        b.copy("dve", gain4[:, 2 + j, :], sm[:, 128:256], r=["sm"], w=[("g4", 2 + j)])
    G4 = [("g4", j) for j in range(4)]
    for j in range(2):
        b.tt("dve", ltmp[:], sm[:, 256 + 256 * j:384 + 256 * j], sm[:, 384 + 256 * j:512 + 256 * j], ALU.mult,
             r=["sm"], w=["ltmp"])
        b.red(ld[:, j:j + 1], ltmp[:], r=["ltmp"], w=["ld"])
    b.act(ld[:], ld[:], AF.Exp, r=["ld"], w=["ld"])
    b.tt("dve", lam[:], ld[:, 0:1], ld[:, 1:2], ALU.subtract, r=["ld"], w=["lam"])
    b.tt("dve", lam[:], lam[:], sm[:, NSMALL - 2:NSMALL - 1], ALU.add, r=["lam", "sm"], w=["lam"])
    c1m = sm[:, NSMALL - 1:NSMALL]
    sgain = sm[:, 768:1024]
    for j in range(2):
        b.memset("pool", mtmp[:, 0:256], 0.0, w=["mtmp"])
        b.asel(mtmp[:, 0:256], mtmp[:, 0:256], [[1, 256]], ALU.is_ge, NEG, -128 * j, -1, r=["mtmp"], w=["mtmp"])
        b.copy("dve", Dm[:, j, :], mtmp[:, 0:256], r=["mtmp"], w=["Dm"])
    for j in range(4):
        b.memset("pool", mtmp[:], 0.0, w=["mtmp"])
        b.asel(mtmp[:], mtmp[:], [[1, 512]], ALU.is_gt, NEG, -128 * j, -1, r=["mtmp"], w=["mtmp"])
        b.copy("dve", Mm[:, j, :], mtmp[:], r=["mtmp"], w=["Mm"])
    b.memset("pool", mtmp[:, 0:128], -1.0, w=["mtmp"])
    b.copy("dve", nones[:], mtmp[:, 0:128], r=["mtmp"], w=["nones"])
    b.asel(mtmp[:, 0:128], mtmp[:, 0:128], [[-1, 128]], ALU.is_ge, 0.0, 0, 1, r=["mtmp"], w=["mtmp"])
    b.copy("dve", ntri[:], mtmp[:, 0:128], r=["mtmp"], w=["ntri"])
    b.memset("pool", V[:, :, 256:257], 1.0, w=["Vones"])

    stg_n = [0]

    def load_weights(c0, ncols, wkeys_prev):
        for p in range(ncols // WPC):
            i = stg_n[0] % 2
            stg_n[0] += 1
            b.load(stg[i][:], wv[:, :, c0 + p * WPC:c0 + (p + 1) * WPC], w=[("stg", i)], sem=f"stg{i}")
            eng = "dve" if p % 2 == 0 else "pool"
            b.copy(eng, WB[:, :, p * WPC:(p + 1) * WPC], stg[i][:], r=[("stg", i)],
                   w=[("WB", p)] + (wkeys_prev if p < 2 else []))
        return [("WB", p) for p in range(ncols // WPC)]

    def wkeys(c0, c1):
        return [("WB", p) for p in range(c0 // WPC, (c1 - 1) // WPC + 1)]

    load_weights(0, WDA, [])
    for ci in range(nchunks):
        hb = HT[ci % 2]
        hk = ("HT", ci % 2)
        b.load(hb[:], hTv[:, :, ci * 512:(ci + 1) * 512], w=[hk], sem=f"HT{ci % 2}")
        b.load(CS[:], csv[:, ci * 4:(ci + 1) * 4, :], w=["CS"], sem="CS")
        qt = QT[ci % 2]
        qk_ = ("QT", ci % 2)
        zs = ZS[ci % 2]
        zk = ("ZS", ci % 2)
        zs3 = zs[:].rearrange("p (a c) -> p a c", a=4)
        for tb in range(4):
            blk = ci * 4 + tb
            for kc in range(KC):
                b.mm(PB[0][:], hb[:, kc, tb * 128:(tb + 1) * 128], WB[:, kc, 0:512], kc == 0, kc == KC - 1,
                     r=[hk] + wkeys(0, 512), w=[BK[0]])
            b.act(T2[0][:], PB[0][:], AF.Square, r=[BK[0]], w=["T2_0"])
            b.red(st4[:, 0:4], T2[0][:].rearrange("p (a c) -> p a c", a=4), r=["T2_0"], w=["q_ss"])
            b.rstd(st4[:, 4:8], st4[:, 0:4], 1.0 / 128, "q")
            t1 = T2[1][:].rearrange("p (a c) -> p a c", a=4)
            b.tt("dve", t1, PB[0][:].rearrange("p (a c) -> p a c", a=4),
                 st4[:, 4:8].unsqueeze(2).broadcast_to([128, 4, 128]), ALU.mult, r=[BK[0], "q_r"], w=["T2_1"])
            b.tt("dve", t1, t1, gain4[:], ALU.mult, r=["T2_1"] + G4, w=["T2_1"])
            t4 = T2[1][:].rearrange("p (a h c) -> p a h c", a=4, h=2)
            x1 = t4[:, :, 0, :]
            x2 = t4[:, :, 1, :]
            cosb = CS[:, tb, 0:64].unsqueeze(1).broadcast_to([128, 4, 64])
            sinb = CS[:, tb, 64:128].unsqueeze(1).broadcast_to([128, 4, 64])
            b.tt("dve", RA[0][:], x1, cosb, ALU.mult, r=["T2_1", "CS"], w=["RA0"])
            b.tt("pool", RA[1][:], x2, sinb, ALU.mult, r=["T2_1", "CS"], w=["RA1"])
            b.tt("dve", RA[2][:], x2, cosb, ALU.mult, r=["T2_1", "CS"], w=["RA2"])
            b.tt("pool", RA[3][:], x1, sinb, ALU.mult, r=["T2_1", "CS"], w=["RA3"])
            q4 = qkr[:].rearrange("p a (h c) -> p a h c", h=2)
            b.tt("dve", q4[:, :, 0, :], RA[0][:], RA[1][:], ALU.subtract, r=["RA0", "RA1"], w=["qkr0"])
            b.tt("dve", q4[:, :, 1, :], RA[2][:], RA[3][:], ALU.add, r=["RA2", "RA3"], w=["qkr1"])
            pT = PB[1][:].bitcast(BF16)[:, 0:512].rearrange("p (a c) -> p a c", a=4)
            for a in range(4):
                b.tr(pT[:, a, :], qkr[:, a, :], ident[:], r=["qkr0", "qkr1", "ident"], w=[BK[1]])
            b.copy("act", qt[:, :, tb * 128:(tb + 1) * 128], pT[:, 0:2, :], r=[BK[1]], w=[qk_])
            b.copy("act", KT[:, :, blk * 128:(blk + 1) * 128], pT[:, 2:4, :], r=[BK[1]],
                   w=[("KT", blk)])
            for kc in range(KC):
                b.mm(PB[0][:], hb[:, kc, tb * 128:(tb + 1) * 128], WB[:, kc, 512:1024], kc == 0, kc == KC - 1,
                     r=[hk] + wkeys(512, 1024), w=[BK[0]])
            b.copy("act", V[:, blk, 0:256], PB[0][:, 0:256], r=[BK[0]], w=[("V", blk)])
            b.act(zs3[:, tb, :], PB[0][:, 256:512], AF.Silu, r=[BK[0]], w=[zk])
        for s in range(2):
            g = 2 * ci + s
            nkb = 2 * g + 2
            OB = [[PB[4 + 2 * m + bb] for bb in range(2)] for m in range(2)]
            def da_a(kb):
                sb_ = PB[2 + kb % 2]
                sk = BK[2 + kb % 2]
                diag = kb >= 2 * g
                for m in range(2):
                    b.mm(sb_[:, m * 256:(m + 1) * 256], KT[:, m, kb * 128:(kb + 1) * 128],
                         qt[:, m, s * 256:(s + 1) * 256], True, not diag, r=[("KT", kb), qk_], w=[sk])
                    if diag:
                        b.mm(sb_[:, m * 256:(m + 1) * 256], ident[:], Dm[:, kb - 2 * g, :], False, True,
                             r=["ident", "Dm"], w=[sk])
                b.act(PW[kb % 3][:], sb_[:], AF.Exp, r=[sk], w=[("PW", kb % 3)])

            def da_e(kb):
                pw = PW[kb % 3]
                for m in range(2):
                    for bb in range(2):
                        b.mm(OB[m][bb][:, 0:257], pw[:, m * 256 + bb * 128:m * 256 + (bb + 1) * 128], V[:, kb, :],
                             kb == 0, kb == nkb - 1, r=[("PW", kb % 3), ("V", kb), "Vones"], w=[BK[4 + 2 * m + bb]])

            for t in range(nkb + 1):
                if t < nkb:
                    da_a(t)
                if t >= 1:
                    da_e(t - 1)
            for bb in range(2):
                qb = 2 * g + bb
                tb = 2 * s + bb
                O1 = OB[0][bb]
                O2 = OB[1][bb]
                b.recip(fo_st[:, 0:1], O1[:, 256:257], r=[BK[4 + bb]], w=["fo_r1"])
                b.recip(fo_st[:, 1:2], O2[:, 256:257], r=[BK[6 + bb]], w=["fo_r2"])
                b.tt("dve", fo_st[:, 1:2], fo_st[:, 1:2], lam[:], ALU.mult, r=["fo_r2", "lam"], w=["fo_r2"])
                b.ts("dve", fo_t[:], O2[:, 0:256], fo_st[:, 1:2], None, ALU.mult, r=[BK[6 + bb], "fo_r2"],
                     w=["fo_t"])
                b.stt("dve", fo_oa[:], O1[:, 0:256], fo_st[:, 0:1], fo_t[:], ALU.mult, ALU.subtract,
                      r=[BK[4 + bb], "fo_r1", "fo_t"], w=["fo_oa"])
                b.act(fo_j[:], fo_oa[:], AF.Square, r=["fo_oa"], w=["fo_j", "s_ss"], accum_out=fo_st[:, 2:3])
                b.rstd(fo_st[:, 3:4], fo_st[:, 2:3], 1.0 / 256, "s")
                b.tt("dve", fo_st[:, 3:4], fo_st[:, 3:4], c1m, ALU.mult, r=["s_r", "sm"], w=["s_r"])
                b.stt("dve", fo_t[:], fo_oa[:], fo_st[:, 3:4], sgain, ALU.mult, ALU.mult, r=["fo_oa", "s_r", "sm"],
                      w=["fo_t"])
                uo = UAo[qb % 2]
                b.tt("dve", uo[:], fo_t[:], zs3[:, tb, :], ALU.mult, r=["fo_t", zk], w=[("UAo", qb % 2)])
                b.store(ua[qb * 128:(qb + 1) * 128, :], uo[:], r=[("UAo", qb % 2)], w=["ua_out"], sem=f"UAo{qb % 2}")

    load_weights(WDA, WSB, BK)
    OTB = [PB[6], PB[7]]
    for ci in range(nchunks):
        hb = HT[ci % 2]
        hk = ("HT", ci % 2)
        b.load(hb[:], hTv[:, :, ci * 512:(ci + 1) * 512], w=[hk], sem=f"HT{ci % 2}")
        qt = QT[ci % 2]
        qk_ = ("QT", ci % 2)
        zs = ZS[ci % 2]
        zk = ("ZS", ci % 2)
        zs2 = zs[:].rearrange("p (a c) -> p a c", a=2)
        for j in range(10):
            pb = PB[j % 2]
            pk = BK[j % 2]
            for kc in range(KC):
                b.mm(pb[:], WB[:, kc, j * 128:(j + 1) * 128], hb[:, kc, :], kc == 0, kc == KC - 1,
                     r=[hk] + wkeys(j * 128, (j + 1) * 128), w=[pk])
            if j < 2:
                b.ts("dve", qt[:, j, :], pb[:], inv, None, ALU.mult, r=[pk], w=[qk_])
            elif j < 4:
                b.copy("act", KT[:, j - 2, ci * 512:(ci + 1) * 512], pb[:], r=[pk],
                       w=[("KT", ci * 4 + t) for t in range(4)])
            elif j < 6:
                b.act(zs2[:, j - 4, :], pb[:], AF.Silu, r=[pk], w=[zk])
            else:
                so = SGo[j % 3]
                b.act(so[:], pb[:], AF.Sigmoid, r=[pk], w=[("SGo", j % 3)])
                b.store(sgT[(j - 6) * 128:(j - 5) * 128, ci * 512:(ci + 1) * 512], so[:], r=[("SGo", j % 3)],
                        w=["sg_out"], sem=f"SGo{j % 3}")
        for tb in range(4):
            blk = ci * 4 + tb
            pb = PB[tb % 2]
            pk = BK[tb % 2]
            for kc in range(KC):
                b.mm(pb[:, 0:256], hb[:, kc, tb * 128:(tb + 1) * 128], WB[:, kc, 1280:1536], kc == 0, kc == KC - 1,
                     r=[hk] + wkeys(1280, 1536), w=[pk])
            b.copy("act", V[:, blk, 0:256], pb[:, 0:256], r=[pk], w=[("V", blk)])
        nkb = 4 * ci + 4
        blocks = []
        for h in range(2):
            for kb in range(nkb - 1, -1, -1):
                blocks.append((h, kb, kb == nkb - 1, len(blocks)))

        def sb_a(blk):
            h, kb, first, xi = blk
            xb = PB[2 + xi % 4]
            xk = BK[2 + xi % 4]
            jd = kb - 4 * ci
            b.mm(xb[:], KT[:, h, kb * 128:(kb + 1) * 128], qt[:, h, :], True, False, r=[("KT", kb), qk_], w=[xk])
            if jd >= 0:
                b.mm(xb[:], ident[:], Mm[:, jd, :], False, False, r=["ident", "Mm"], w=[xk])
            b.act(T2[xi % 2][:], xb[:], AF.Exp, r=[xk], w=[f"T2_{xi % 2}"])
            b.act(SP[xi % 3][:], T2[xi % 2][:], AF.Ln, r=[f"T2_{xi % 2}"], w=[("SP", xi % 3)], bias=1.0)

        def sb_c(blk):
            h, kb, first, xi = blk
            xb = PB[2 + xi % 4]
            xk = BK[2 + xi % 4]
            sp = SP[xi % 3]
            spk = ("SP", xi % 3)
            b.mm(xb[:], ntri[:], sp[:], False, first, r=["ntri", spk], w=[xk])
            if not first:
                b.mm(xb[:], nones[:], RACC[:], False, True, r=["nones", "RACC"], w=[xk])
            if kb > 0:
                if first:
                    b.copy("dve", RACC[:], sp[:], r=[spk], w=["RACC"])
                else:
                    b.tt("dve", RACC[:], RACC[:], sp[:], ALU.add, r=[spk, "RACC"], w=["RACC"])
            b.act(PW[xi % 3][:], xb[:], AF.Exp, r=[xk], w=[("PW", xi % 3)])

        def sb_e(blk):
            h, kb, first, xi = blk
            b.mm(OTB[h][:], V[:, kb, h * 128:(h + 1) * 128], PW[xi % 3][:], first, kb == 0,
                 r=[("V", kb), ("PW", xi % 3)], w=[BK[6 + h]])

        nb_ = len(blocks)
        for t in range(nb_ + 2):
            if t < nb_:
                sb_a(blocks[t])
            if 0 <= t - 1 < nb_:
                sb_c(blocks[t - 1])
            if 0 <= t - 2 < nb_:
                sb_e(blocks[t - 2])
        for h in range(2):
            uo = UBo[h]
            b.tt("dve", uo[:], OTB[h][:], zs2[:, h, :], ALU.mult, r=[BK[6 + h], zk], w=[("UBo", h)])
            b.store(ubT[h * 128:(h + 1) * 128, ci * 512:(ci + 1) * 512], uo[:], r=[("UBo", h)], w=["ub_out"],
                    sem=f"UBo{h}")
    b.finish()
    return nc


def build_phase_c(merge=True, norm=True):
    nc = bass.Bass("TRN2", target_bir_lowering=False)
    b = B(nc)
    T = TPC
    NB = T // 128
    x = nc.dram_tensor("x", [T, D], F32, kind="ExternalInput").ap()
    ng = nc.dram_tensor("ng", [128, D], F32, kind="ExternalInput").ap()
    if merge:
        uaD = nc.dram_tensor("ua", [T, D], BF16, kind="ExternalInput").ap()
        ubD = nc.dram_tensor("ubT", [D, T], BF16, kind="ExternalInput").ap()
        sgaD = nc.dram_tensor("sgaT", [D, T], BF16, kind="ExternalInput").ap()
        sgbD = nc.dram_tensor("sgbT", [D, T], BF16, kind="ExternalInput").ap()
        waD = nc.dram_tensor("wa", [D, D], F32, kind="ExternalInput").ap()
        wbD = nc.dram_tensor("wb", [D, D], F32, kind="ExternalInput").ap()
        woD = nc.dram_tensor("wo", [D, D], F32, kind="ExternalInput").ap()
        xo = nc.dram_tensor("xo", [T, D], F32, kind="ExternalOutput").ap()
    if norm:
        hTo = nc.dram_tensor("hTo", [D, T], BF16, kind="ExternalOutput").ap()
        hTov = hTo.rearrange("(kc p) t -> p kc t", p=128)
    ident = make_ident(b)
    outs = []
    XB = [b.sb(f"XB{i}", [128, D], F32) for i in range(2)]
    if norm:
        NG = b.sb("NG", [128, D], F32)
        b.load(NG[:], ng, w=["NG"], sem="NG")
        junk = b.sb("junk", [128, D], BF16)
        nst = b.sb("nst", [128, 2], F32)
        HB = b.sb("HB", [128, D], BF16)
        HTO = [b.sb(f"HTO{i}", [128, KC, 512], BF16) for i in range(1)]
        PT = [b.ps(f"PT{i}", [128, 4, 128], BF16) for i in range(2)]
    if merge:
        BIG = b.sb("BIG", [128, 2, KC, T], BF16)
        YT = b.sb("YT", [128, KC, T], BF16)
        UAB = [b.sb(f"UAB{i}", [128, D], BF16) for i in range(1)]
        WS = [b.sb(f"WS{i}", [128, KC, 128], F32) for i in range(2)]
        WBF = [b.sb(f"WBF{i}", [128, KC, 128], BF16) for i in range(4)]
        SG = [b.sb(f"SG{i}", [128, 2, T], BF16) for i in range(2)]
        YA = [b.sb(f"YA{i}", [128, 512], F32) for i in range(2)]
        YB = [b.sb(f"YB{i}", [128, 512], F32) for i in range(2)]
        PA = [b.ps(f"PA{i}", [128, 512]) for i in range(4)]
        PTm = [b.ps(f"PTm{i}", [128, 4, 128], BF16) for i in range(2)]
        for kc in range(KC):
            b.load(BIG[:, 1, kc, :], ubD[kc * 128:(kc + 1) * 128, :], w=[("ubT", kc)], sem="ubT")
        for nb in range(NB):
            ub_ = UAB[0]
            b.load(ub_[:], uaD[nb * 128:(nb + 1) * 128, :], w=[("UAB", 0)], sem="UAB0")
            for q4 in range(4):
                pt = PTm[q4 % 2]
                for a in range(4):
                    kc = q4 * 4 + a
                    b.tr(pt[:, a, :], ub_[:, kc * 128:(kc + 1) * 128], ident[:], r=[("UAB", 0), "ident"],
                         w=[("PTm", q4 % 2, a)])
                b.copy("act" if q4 % 2 else "dve", BIG[:, 0, q4 * 4:(q4 + 1) * 4, nb * 128:(nb + 1) * 128], pt[:],
                       r=[("PTm", q4 % 2, a) for a in range(4)], w=[("uaT", nb)])
        UAT = [("uaT", nb) for nb in range(NB)]
        UBT = [("ubT", kc) for kc in range(KC)]
        for fo in range(KC):
            i2 = fo % 2
            for br, wD in enumerate((waD, wbD)):
                si = 2 * i2 + br
                b.load(WS[br][:], wD.rearrange("(kc p) n -> p kc n", p=128)[:, :, fo * 128:(fo + 1) * 128],
                       w=[("WS", br)], sem=f"WS{br}")
                b.copy("pool" if br else "dve", WBF[si][:], WS[br][:], r=[("WS", br)], w=[("WBF", si)])
            b.load(SG[i2][:, 0, :], sgaD[fo * 128:(fo + 1) * 128, :], w=[("SG", i2, 0)], sem=f"SGa{i2}")
            b.load(SG[i2][:, 1, :], sgbD[fo * 128:(fo + 1) * 128, :], w=[("SG", i2, 1)], sem=f"SGb{i2}")
            for th in range(T // 512):
                for br in range(2):
                    pa = PA[2 * (th % 2) + br]
                    pk = ("PA", 2 * (th % 2) + br)
                    si = 2 * i2 + br
                    for kc in range(KC):
                        b.mm(pa[:], WBF[si][:, kc, :], BIG[:, br, kc, th * 512:(th + 1) * 512], kc == 0, kc == KC - 1,
                             r=[("WBF", si)] + (UAT if br == 0 else UBT), w=[pk])
                ya = YA[th % 2]
                yk = ("YA", th % 2)
                b.tt("dve", ya[:], PA[2 * (th % 2)][:], SG[i2][:, 0, th * 512:(th + 1) * 512], ALU.mult,
                     r=[("PA", 2 * (th % 2)), ("SG", i2, 0)], w=[yk])
                yb = YB[th % 2]
                ybk = ("YB", th % 2)
                b.tt("dve", yb[:], PA[2 * (th % 2) + 1][:],
                     SG[i2][:, 1, th * 512:(th + 1) * 512], ALU.mult, r=[("PA", 2 * (th % 2) + 1), ("SG", i2, 1)],
                     w=[ybk])
                b.tt("pool", YT[:, fo, th * 512:(th + 1) * 512], yb[:], ya[:], ALU.add,
                     r=[ybk, yk], w=[("YT", fo, th)])
        YTK = [("YT", fo, th) for fo in range(KC) for th in range(T // 512)]
        WO = BIG[:].rearrange("p a k t -> p (a k t)").rearrange("p (k n) -> p k n", k=KC)
        for fo in range(KC):
            si = fo % 2
            b.load(WS[si][:], woD.rearrange("(kc p) n -> p kc n", p=128)[:, :, fo * 128:(fo + 1) * 128],
                   w=[("WS", si)], sem=f"WS{si}")
            b.copy("pool" if fo % 2 else "dve", WO[:, :, fo * 128:(fo + 1) * 128], WS[si][:], r=[("WS", si)],
                   w=[("WO", fo)] + UAT + UBT)
    for nb in range(NB):
        xb = XB[nb % 2]
        xk = ("XB", nb % 2)
        b.load(xb[:], x[nb * 128:(nb + 1) * 128, :], w=[xk], sem=f"XB{nb % 2}")
        if merge:
            for cg in range(4):
                pa = PA[cg]
                pk = ("PA", cg)
                for kc in range(KC):
                    b.mm(pa[:], YT[:, kc, nb * 128:(nb + 1) * 128], WO[:, kc, cg * 512:(cg + 1) * 512], kc == 0,
                         kc == KC - 1, r=YTK + [("WO", f) for f in range(cg * 4, cg * 4 + 4)], w=[pk])
                b.tt("dve", xb[:, cg * 512:(cg + 1) * 512], xb[:, cg * 512:(cg + 1) * 512], pa[:], ALU.add,
                     r=[xk, pk], w=[xk])
            b.store(xo[nb * 128:(nb + 1) * 128, :], xb[:], r=[xk], w=["xo_out"], sem=f"XO{nb % 2}")
            outs.append("xo_out")
        if norm:
            b.act(junk[:], xb[:], AF.Square, r=[xk], w=["junk", "n_ss"], accum_out=nst[:, 0:1])
            b.rstd(nst[:, 1:2], nst[:, 0:1], 1.0 / D, "n")
            b.stt("dve", HB[:], xb[:], nst[:, 1:2], NG[:], ALU.mult, ALU.mult, r=[xk, "n_r", "NG"], w=["HB"])
            ho = HTO[0]
            hok = ("HTO", 0)
            for q4 in range(4):
                pt = PT[q4 % 2]
                for a in range(4):
                    kc = q4 * 4 + a
                    b.tr(pt[:, a, :], HB[:, kc * 128:(kc + 1) * 128], ident[:], r=["HB", "ident"],
                         w=[("PT", q4 % 2, a)])
                b.copy("act" if q4 % 2 else "dve", ho[:, q4 * 4:(q4 + 1) * 4, (nb % 4) * 128:(nb % 4 + 1) * 128], pt[:],
                       r=[("PT", q4 % 2, a) for a in range(4)], w=[hok])
            if nb % 4 == 3:
                c4 = nb // 4
                b.store(hTov[:, :, c4 * 512:(c4 + 1) * 512], ho[:], r=[hok], w=["hT_out"], sem="HTO0")
                outs.append("hT_out")
    b.finish()
    return nc


_CACHE = {}


def _prog(name, fn):
    if name not in _CACHE:
        _CACHE[name] = fn()
    return _CACHE[name]


def _run(nc, maps):
    res = run_bass_kernel_spmd(nc, maps, core_ids=list(range(NCORES)))
    return res.results


def _rope_table():
    d = 128
    inv_freq = np.exp(-(np.arange(0, d, 2, dtype=np.float32) / d) * math.log(10000.0)).astype(np.float32)
    ang = np.arange(SEQ, dtype=np.float32)[:, None] * inv_freq[None, :]
    return np.concatenate([np.cos(ang), np.sin(ang)], axis=1).astype(np.float32)


def _w_cols(c):
    seg = 2048
    r = lambda s, a, n: np.arange(s * seg + a, s * seg + a + n)
    cols = [r(0, 256 * c, 256), r(1, 256 * c, 256), r(2, 256 * c, 256), r(3, 256 * c, 256),
            r(4, 256 * c, 256), r(5, 256 * c, 256), r(7, 256 * c, 256),
            r(8, 256 * c, 256), r(9, 256 * c, 256), r(6, 256 * c, 256)]
    return np.concatenate(cols)


def kernel(x, norm_gain, w_in, qk_q_gain, qk_k_gain, lambda_q1, lambda_k1, lambda_q2, lambda_k2,
           subln_gain, w_branch_a, w_branch_b, w_out):
    bf = ml_dtypes.bfloat16
    rep = lambda v: np.ascontiguousarray(np.broadcast_to(np.asarray(v, np.float32)[None, :], (128, v.shape[-1])))
    xs = [np.ascontiguousarray(x[0, c * TPC:(c + 1) * TPC, :]) for c in range(NCORES)]
    cs = _rope_table()
    pn = _prog("n", lambda: build_phase_c(merge=False, norm=True))
    pab = _prog("ab", build_phase_ab)
    pc = _prog("c", lambda: build_phase_c(merge=True, norm=True))
    pcl = _prog("cl", lambda: build_phase_c(merge=True, norm=False))
    res = _run(pn, [{"x": xs[c], "ng": rep(norm_gain[0])} for c in range(NCORES)])
    hT = np.concatenate([res[c]["hTo"] for c in range(NCORES)], axis=1)
    for l in range(DEPTH):
        li = 0.8 - 0.6 * math.exp(-0.3 * l)
        small = np.concatenate([rep(qk_q_gain[l]), rep(qk_k_gain[l]), rep(lambda_q1[l]), rep(lambda_k1[l]),
                                rep(lambda_q2[l]), rep(lambda_k2[l]), rep(subln_gain[l]),
                                np.full((128, 1), li, np.float32), np.full((128, 1), 1.0 - li, np.float32)], axis=1)
        maps = [{"hT": hT, "w": np.ascontiguousarray(w_in[l][:, _w_cols(c)]), "cs": cs, "small": small}
                for c in range(NCORES)]
        res = _run(pab, maps)
        ua = np.concatenate([res[c]["ua"] for c in range(NCORES)], axis=1)
        ubT = np.concatenate([res[c]["ubT"] for c in range(NCORES)], axis=0)
        sgaT = np.concatenate([res[c]["sgT"][0:256] for c in range(NCORES)], axis=0)
        sgbT = np.concatenate([res[c]["sgT"][256:512] for c in range(NCORES)], axis=0)
        last = l == DEPTH - 1
        maps = []
        for c in range(NCORES):
            t0, t1 = c * TPC, (c + 1) * TPC
            maps.append({"x": xs[c], "ng": rep(norm_gain[min(l + 1, DEPTH - 1)]),
                         "ua": np.ascontiguousarray(ua[t0:t1]), "ubT": np.ascontiguousarray(ubT[:, t0:t1]),
                         "sgaT": np.ascontiguousarray(sgaT[:, t0:t1]), "sgbT": np.ascontiguousarray(sgbT[:, t0:t1]),
                         "wa": w_branch_a[l], "wb": w_branch_b[l], "wo": w_out[l]})
        res = _run(pcl if last else pc, maps)
        xs = [res[c]["xo"] for c in range(NCORES)]
        if not last:
            hT = np.concatenate([res[c]["hTo"] for c in range(NCORES)], axis=1)
    return np.concatenate(xs, axis=0)[None].astype(np.float32)
```

```python
import math
import numpy as np
import ml_dtypes
import concourse.bass as bass
import concourse.mybir as mybir
from concourse.bass_utils import run_bass_kernel_spmd

F32 = mybir.dt.float32
BF16 = mybir.dt.bfloat16
AF = mybir.ActivationFunctionType
ALU = mybir.AluOpType
AX = mybir.AxisListType

NCORES = 8
D = 2048
SEQ = 8192
DEPTH = 4
KC = 16
TPC = SEQ // NCORES
NEG = -30000.0
RMS_EPS = 1e-6

EP = 12000
EPD = 1000


class Op:
    __slots__ = ("eng", "fn", "deps", "is_dma", "sem", "semval", "signal", "count", "idx")


class Sched:
    ENG = ("pe", "act", "dve", "pool", "sp")

    def __init__(self, nc):
        self.nc = nc
        self.q = {e: [] for e in self.ENG}
        self.last_w = {}
        self.readers = {}
        self.dma_cnt = {}
        self.last_dma = {}
        self.n = 0

    @staticmethod
    def _src(op):
        return ("d", op.sem) if op.is_dma else ("e", op.eng)

    def add(self, eng, fn, reads=(), writes=(), dma=None):
        op = Op()
        op.eng = eng
        op.fn = fn
        op.is_dma = dma is not None
        op.signal = False
        op.count = 0
        op.idx = self.n
        self.n += 1
        deps = {}

        def dep(o):
            s = self._src(o)
            cur = deps.get(s)
            if cur is None or o.idx > cur.idx:
                deps[s] = o

        for k in reads:
            w = self.last_w.get(k)
            if w is not None:
                dep(w)
        for k in writes:
            w = self.last_w.get(k)
            if w is not None:
                dep(w)
            for r in self.readers.get(k, {}).values():
                dep(r)
        op.deps = list(deps.values())
        if dma is not None:
            c = self.dma_cnt.get(dma, 0) + 1
            self.dma_cnt[dma] = c
            op.sem = dma
            op.semval = c
        else:
            op.sem = None
            op.semval = 0
        for k in reads:
            self.readers.setdefault(k, {})[self._src(op)] = op
        for k in writes:
            self.last_w[k] = op
            self.readers[k] = {}
        self.q[eng].append(op)
        if dma is not None:
            self.last_dma[dma] = op
        return op

    def emit(self):
        nc = self.nc
        for e in self.ENG:
            for op in self.q[e]:
                for d in op.deps:
                    if d.is_dma:
                        continue
                    if d.eng == "pe" and op.eng == "pe":
                        continue
                    d.signal = True
        nsig = {}
        for e in self.ENG:
            c = 0
            for op in self.q[e]:
                if op.signal and not op.is_dma:
                    c += 1
                    op.count = c
            nsig[e] = c
        self.esem = {}
        for e in self.ENG:
            for ep in range((nsig[e] + EP - 1) // EP):
                self.esem[(e, ep)] = nc.alloc_semaphore(f"s_{e}_{ep}")
        self.dsem = {}
        for name, c in self.dma_cnt.items():
            for ep in range((c + EPD - 1) // EPD):
                self.dsem[(name, ep)] = nc.alloc_semaphore(f"d_{name}_{ep}")
        allsems = list(self.esem.values()) + list(self.dsem.values())
        for sh in allsems:
            nc.gpsimd.sem_clear(sh)
        nc.all_engine_barrier()
        with nc.Block() as block:
            @block.tensor
            def _(eng):
                self._emit_engine("pe", eng)

            @block.scalar
            def _(eng):
                self._emit_engine("act", eng)

            @block.vector
            def _(eng):
                self._emit_engine("dve", eng)

            @block.gpsimd
            def _(eng):
                self._emit_engine("pool", eng)

            @block.sync
            def _(eng):
                self._emit_engine("sp", eng)
        nc.all_engine_barrier()
        for sh in allsems:
            nc.gpsimd.sem_clear(sh)
        nc.all_engine_barrier()

    def _emit_engine(self, e, eng):
        waited = {}
        for op in self.q[e]:
            for d in op.deps:
                if d.is_dma:
                    key = ("d", d.sem)
                    val = d.semval
                    if waited.get(key, 0) >= val:
                        continue
                    waited[key] = val
                    ep = (val - 1) // EPD
                    eng.wait_ge(self.dsem[(d.sem, ep)], 16 * (val - ep * EPD))
                else:
                    if d.eng == "pe" and e == "pe":
                        continue
                    key = ("e", d.eng)
                    val = d.count
                    if waited.get(key, 0) >= val:
                        continue
                    waited[key] = val
                    ep = (val - 1) // EP
                    eng.wait_ge(self.esem[(d.eng, ep)], val - ep * EP)
            ins = op.fn(eng)
            if op.is_dma:
                ep = (op.semval - 1) // EPD
                ins.then_inc(self.dsem[(op.sem, ep)], 16)
            elif op.signal:
                ep = (op.count - 1) // EP
                ins.then_inc(self.esem[(e, ep)], 1)


class B:
    def __init__(self, nc):
        self.nc = nc
        self.S = Sched(nc)

    def sb(self, name, shape, dt):
        return self.nc.alloc_sbuf_tensor(name, list(shape), dt)

    def ps(self, name, shape, dt=F32):
        return self.nc.alloc_psum_tensor(name, list(shape), dt)

    def load(self, out, in_, r=(), w=(), sem=None, q="sp"):
        return self.S.add(q, lambda e: e.dma_start(out=out, in_=in_), r, w, dma=sem)

    def store(self, out, in_, r=(), w=(), sem=None, q="pool"):
        return self.S.add(q, lambda e: e.dma_start(out=out, in_=in_), r, w, dma=sem)

    def mm(self, out, lhsT, rhs, start, stop, r=(), w=(), nocheck=False):
        if nocheck:
            return self.S.add("pe", lambda e: e.matmul(out, lhsT=lhsT, rhs=rhs, start=start, stop=stop,
                                                       skip_group_check=True), r, w)
        return self.S.add("pe", lambda e: e.matmul(out, lhsT=lhsT, rhs=rhs, start=start, stop=stop), r, w)

    def tr(self, out, in_, ident, r=(), w=()):
        return self.S.add("pe", lambda e: e.transpose(out=out, in_=in_, identity=ident), r, w)

    def act(self, out, in_, func, r=(), w=(), **kw):
        return self.S.add("act", lambda e: e.activation(out=out, in_=in_, func=func, **kw), r, w)

    def copy(self, eng, out, in_, r=(), w=()):
        if eng == "act":
            return self.S.add("act", lambda e: e.copy(out=out, in_=in_), r, w)
        return self.S.add(eng, lambda e: e.tensor_copy(out=out, in_=in_), r, w)

    def tt(self, eng, out, in0, in1, op, r=(), w=()):
        return self.S.add(eng, lambda e: e.tensor_tensor(out=out, in0=in0, in1=in1, op=op), r, w)

    def ts(self, eng, out, in0, s1, s2, op0, op1=None, r=(), w=()):
        if op1 is None:
            return self.S.add(eng, lambda e: e.tensor_scalar(out=out, in0=in0, scalar1=s1, scalar2=None, op0=op0), r, w)
        return self.S.add(eng, lambda e: e.tensor_scalar(out=out, in0=in0, scalar1=s1, scalar2=s2, op0=op0, op1=op1), r, w)

    def stt(self, eng, out, in0, scalar, in1, op0, op1, r=(), w=()):
        return self.S.add(eng, lambda e: e.scalar_tensor_tensor(out=out, in0=in0, scalar=scalar, in1=in1, op0=op0, op1=op1), r, w)

    def red(self, out, in_, r=(), w=()):
        return self.S.add("dve", lambda e: e.reduce_sum(out=out, in_=in_, axis=AX.X), r, w)

    def recip(self, out, in_, r=(), w=()):
        return self.S.add("dve", lambda e: e.reciprocal(out=out, in_=in_), r, w)

    def memset(self, eng, ap, val, r=(), w=()):
        return self.S.add(eng, lambda e: e.memset(ap, val), r, w)

    def asel(self, out, in_, pattern, cmp, fill, base, cm, r=(), w=()):
        return self.S.add("pool", lambda e: e.affine_select(out=out, in_=in_, pattern=pattern, compare_op=cmp,
                                                           fill=fill, base=base, channel_multiplier=cm), r, w)

    def rstd(self, out, ss, scale, tag):
        self.ts("dve", out, ss, scale, RMS_EPS, ALU.mult, ALU.add, r=[tag + "_ss"], w=[tag + "_r"])
        self.act(out, out, AF.Sqrt, r=[tag + "_r"], w=[tag + "_r"])
        self.recip(out, out, r=[tag + "_r"], w=[tag + "_r"])

    def finish(self):
        for q in ("sp", "pool"):
            op = self.S.add(q, lambda e: e.nop(), (), ())
            op.deps = list(self.S.last_dma.values())
        self.S.emit()


def make_ident(b, name="ident"):
    idf = b.sb(name + "_f", [128, 128], F32)
    idb = b.sb(name, [128, 128], BF16)
    b.memset("pool", idf[:], 0.0, w=[name + "_f"])
    b.asel(idf[:], idf[:], [[-1, 128]], ALU.not_equal, 1.0, 0, 1, r=[name + "_f"], w=[name + "_f"])
    b.copy("dve", idb[:], idf[:], r=[name + "_f"], w=[name])
    return idb


NSMALL = 6 * 128 + 256 + 2
WDA = 1024
WSB = 1536
WPC = 64


def build_phase_ab(nchunks=16):
    nc = bass.Bass("TRN2", target_bir_lowering=False)
    b = B(nc)
    S = b.S
    ntok = nchunks * 512
    hT = nc.dram_tensor("hT", [D, ntok], BF16, kind="ExternalInput").ap()
    w = nc.dram_tensor("w", [D, WDA + WSB], F32, kind="ExternalInput").ap()
    cs = nc.dram_tensor("cs", [ntok, 128], F32, kind="ExternalInput").ap()
    small = nc.dram_tensor("small", [128, NSMALL], F32, kind="ExternalInput").ap()
    ua = nc.dram_tensor("ua", [ntok, 256], BF16, kind="ExternalOutput").ap()
    ubT = nc.dram_tensor("ubT", [256, ntok], BF16, kind="ExternalOutput").ap()
    sgT = nc.dram_tensor("sgT", [512, ntok], BF16, kind="ExternalOutput").ap()

    hTv = hT.rearrange("(kc p) t -> p kc t", p=128)
    wv = w.rearrange("(kc p) n -> p kc n", p=128)
    csv = cs.rearrange("(n p) c -> p n c", p=128)

    WB = b.sb("WB", [128, KC, WSB], BF16)
    stg = [b.sb(f"stg{i}", [128, KC, WPC], F32) for i in range(2)]
    KT = b.sb("KT", [128, 2, ntok], BF16)
    V = b.sb("V", [128, ntok // 128, 257], BF16)
    HT = [b.sb(f"HT{i}", [128, KC, 512], BF16) for i in range(2)]
    sm = b.sb("sm", [128, NSMALL], F32)
    gain4 = b.sb("gain4", [128, 4, 128], F32)
    lam = b.sb("lam", [128, 1], F32)
    ltmp = b.sb("ltmp", [128, 128], F32)
    ld = b.sb("ld", [128, 2], F32)
    mtmp = b.sb("mtmp", [128, 512], F32)
    Dm = b.sb("Dm", [128, 2, 256], BF16)
    Mm = b.sb("Mm", [128, 4, 512], BF16)
    ntri = b.sb("ntri", [128, 128], BF16)
    nones = b.sb("nones", [128, 128], BF16)
    CS = b.sb("CS", [128, 4, 128], F32)
    T2 = [b.sb(f"T2_{i}", [128, 512], F32) for i in range(2)]
    RA = [b.sb(f"RA{i}", [128, 4, 64], F32) for i in range(4)]
    qkr = b.sb("qkr", [128, 4, 128], BF16)
    QT = [b.sb(f"QT{i}", [128, 2, 512], BF16) for i in range(2)]
    ZS = [b.sb(f"ZS{i}", [128, 1024], BF16) for i in range(2)]
    PW = [b.sb(f"PW{i}", [128, 512], BF16) for i in range(3)]
    SP = [b.sb(f"SP{i}", [128, 512], BF16) for i in range(3)]
    RACC = b.sb("RACC", [128, 512], BF16)
    st4 = b.sb("st4", [128, 8], F32)
    stq = b.sb("stq", [128, 8], F32)
    fo_t = b.sb("fo_t", [128, 256], F32)
    fo_oa = b.sb("fo_oa", [128, 256], F32)
    fo_j = b.sb("fo_j", [128, 256], F32)
    fo_st = b.sb("fo_st", [128, 8], F32)
    UAo = [b.sb(f"UAo{i}", [128, 256], BF16) for i in range(2)]
    SGo = [b.sb(f"SGo{i}", [128, 512], BF16) for i in range(3)]
    UBo = [b.sb(f"UBo{i}", [128, 512], BF16) for i in range(2)]

    PB = [b.ps(f"PB{i}", [128, 512]) for i in range(8)]
    BK = [("PB", i) for i in range(8)]

    ident = make_ident(b)

    b.load(sm[:], small, w=["sm"], sem="sm")
    inv = 128.0 ** -0.5
    for j in range(2):
        b.ts("dve", gain4[:, j, :], sm[:, 0:128], inv, None, ALU.mult, r=["sm"], w=[("g4", j)])
        b.copy("dve", gain4[:, 2 + j, :], sm[:, 128:256], r=["sm"], w=[("g4", 2 + j)])
    G4 = [("g4", j) for j in range(4)]
    for j in range(2):
        b.tt("dve", ltmp[:], sm[:, 256 + 256 * j:384 + 256 * j], sm[:, 384 + 256 * j:512 + 256 * j], ALU.mult,
             r=["sm"], w=["ltmp"])
        b.red(ld[:, j:j + 1], ltmp[:], r=["ltmp"], w=["ld"])
    b.act(ld[:], ld[:], AF.Exp, r=["ld"], w=["ld"])
    b.tt("dve", lam[:], ld[:, 0:1], ld[:, 1:2], ALU.subtract, r=["ld"], w=["lam"])
    b.tt("dve", lam[:], lam[:], sm[:, NSMALL - 2:NSMALL - 1], ALU.add, r=["lam", "sm"], w=["lam"])
    c1m = sm[:, NSMALL - 1:NSMALL]
    sgain = sm[:, 768:1024]
    for j in range(2):
        b.memset("pool", mtmp[:, 0:256], 0.0, w=["mtmp"])
        b.asel(mtmp[:, 0:256], mtmp[:, 0:256], [[1, 256]], ALU.is_ge, NEG, -128 * j, -1, r=["mtmp"], w=["mtmp"])
        b.copy("dve", Dm[:, j, :], mtmp[:, 0:256], r=["mtmp"], w=["Dm"])
    for j in range(4):
        b.memset("pool", mtmp[:], 0.0, w=["mtmp"])
        b.asel(mtmp[:], mtmp[:], [[1, 512]], ALU.is_gt, NEG, -128 * j, -1, r=["mtmp"], w=["mtmp"])
        b.copy("dve", Mm[:, j, :], mtmp[:], r=["mtmp"], w=["Mm"])
    b.memset("pool", mtmp[:, 0:128], -1.0, w=["mtmp"])
    b.copy("dve", nones[:], mtmp[:, 0:128], r=["mtmp"], w=["nones"])
    b.asel(mtmp[:, 0:128], mtmp[:, 0:128], [[-1, 128]], ALU.is_ge, 0.0, 0, 1, r=["mtmp"], w=["mtmp"])
    b.copy("dve", ntri[:], mtmp[:, 0:128], r=["mtmp"], w=["ntri"])
    b.memset("pool", V[:, :, 256:257], 1.0, w=["Vones"])

    stg_n = [0]

    def load_weights(c0, ncols, wkeys_prev):
        for p in range(ncols // WPC):
            i = stg_n[0] % 2
            stg_n[0] += 1
            b.load(stg[i][:], wv[:, :, c0 + p * WPC:c0 + (p + 1) * WPC], w=[("stg", i)], sem=f"stg{i}")
            eng = "dve" if p % 2 == 0 else "pool"
            b.copy(eng, WB[:, :, p * WPC:(p + 1) * WPC], stg[i][:], r=[("stg", i)],
                   w=[("WB", p)] + (wkeys_prev if p < 2 else []))
        return [("WB", p) for p in range(ncols // WPC)]

    def wkeys(c0, c1):
        return [("WB", p) for p in range(c0 // WPC, (c1 - 1) // WPC + 1)]

    load_weights(0, WDA, [])
    for ci in range(nchunks):
        hb = HT[ci % 2]
        hk = ("HT", ci % 2)
        b.load(hb[:], hTv[:, :, ci * 512:(ci + 1) * 512], w=[hk], sem=f"HT{ci % 2}")
        b.load(CS[:], csv[:, ci * 4:(ci + 1) * 4, :], w=["CS"], sem="CS")
        qt = QT[ci % 2]
        qk_ = ("QT", ci % 2)
        zs = ZS[ci % 2]
        zk = ("ZS", ci % 2)
        zs3 = zs[:].rearrange("p (a c) -> p a c", a=4)
        for tb in range(4):
            blk = ci * 4 + tb
            for kc in range(KC):
                b.mm(PB[0][:], hb[:, kc, tb * 128:(tb + 1) * 128], WB[:, kc, 0:512], kc == 0, kc == KC - 1,
                     r=[hk] + wkeys(0, 512), w=[BK[0]])
            b.act(T2[0][:], PB[0][:], AF.Square, r=[BK[0]], w=["T2_0"])
            b.red(st4[:, 0:4], T2[0][:].rearrange("p (a c) -> p a c", a=4), r=["T2_0"], w=["q_ss"])
            b.rstd(st4[:, 4:8], st4[:, 0:4], 1.0 / 128, "q")
            t1 = T2[1][:].rearrange("p (a c) -> p a c", a=4)
            b.tt("dve", t1, PB[0][:].rearrange("p (a c) -> p a c", a=4),
                 st4[:, 4:8].unsqueeze(2).broadcast_to([128, 4, 128]), ALU.mult, r=[BK[0], "q_r"], w=["T2_1"])
            b.tt("dve", t1, t1, gain4[:], ALU.mult, r=["T2_1"] + G4, w=["T2_1"])
            t4 = T2[1][:].rearrange("p (a h c) -> p a h c", a=4, h=2)
            x1 = t4[:, :, 0, :]
            x2 = t4[:, :, 1, :]
            cosb = CS[:, tb, 0:64].unsqueeze(1).broadcast_to([128, 4, 64])
            sinb = CS[:, tb, 64:128].unsqueeze(1).broadcast_to([128, 4, 64])
            b.tt("dve", RA[0][:], x1, cosb, ALU.mult, r=["T2_1", "CS"], w=["RA0"])
            b.tt("pool", RA[1][:], x2, sinb, ALU.mult, r=["T2_1", "CS"], w=["RA1"])
            b.tt("dve", RA[2][:], x2, cosb, ALU.mult, r=["T2_1", "CS"], w=["RA2"])
            b.tt("pool", RA[3][:], x1, sinb, ALU.mult, r=["T2_1", "CS"], w=["RA3"])
            q4 = qkr[:].rearrange("p a (h c) -> p a h c", h=2)
            b.tt("dve", q4[:, :, 0, :], RA[0][:], RA[1][:], ALU.subtract, r=["RA0", "RA1"], w=["qkr0"])
            b.tt("dve", q4[:, :, 1, :], RA[2][:], RA[3][:], ALU.add, r=["RA2", "RA3"], w=["qkr1"])
            pT = PB[1][:].bitcast(BF16)[:, 0:512].rearrange("p (a c) -> p a c", a=4)
            for a in range(4):
                b.tr(pT[:, a, :], qkr[:, a, :], ident[:], r=["qkr0", "qkr1", "ident"], w=[BK[1]])
            b.copy("act", qt[:, :, tb * 128:(tb + 1) * 128], pT[:, 0:2, :], r=[BK[1]], w=[qk_])
            b.copy("act", KT[:, :, blk * 128:(blk + 1) * 128], pT[:, 2:4, :], r=[BK[1]],
                   w=[("KT", blk)])
            for kc in range(KC):
                b.mm(PB[0][:], hb[:, kc, tb * 128:(tb + 1) * 128], WB[:, kc, 512:1024], kc == 0, kc == KC - 1,
                     r=[hk] + wkeys(512, 1024), w=[BK[0]])
            b.copy("act", V[:, blk, 0:256], PB[0][:, 0:256], r=[BK[0]], w=[("V", blk)])
            b.act(zs3[:, tb, :], PB[0][:, 256:512], AF.Silu, r=[BK[0]], w=[zk])
        for s in range(2):
            g = 2 * ci + s
            nkb = 2 * g + 2
            OB = [[PB[4 + 2 * m + bb] for bb in range(2)] for m in range(2)]
            def da_a(kb):
                sb_ = PB[2 + kb % 2]
                sk = BK[2 + kb % 2]
                diag = kb >= 2 * g
                for m in range(2):
                    b.mm(sb_[:, m * 256:(m + 1) * 256], KT[:, m, kb * 128:(kb + 1) * 128],
                         qt[:, m, s * 256:(s + 1) * 256], True, not diag, r=[("KT", kb), qk_], w=[sk])
                    if diag:
                        b.mm(sb_[:, m * 256:(m + 1) * 256], ident[:], Dm[:, kb - 2 * g, :], False, True,
                             r=["ident", "Dm"], w=[sk])
                b.act(PW[kb % 3][:], sb_[:], AF.Exp, r=[sk], w=[("PW", kb % 3)])

            def da_e(kb):
                pw = PW[kb % 3]
                for m in range(2):
                    for bb in range(2):
                        b.mm(OB[m][bb][:, 0:257], pw[:, m * 256 + bb * 128:m * 256 + (bb + 1) * 128], V[:, kb, :],
                             kb == 0, kb == nkb - 1, r=[("PW", kb % 3), ("V", kb), "Vones"], w=[BK[4 + 2 * m + bb]])

            for t in range(nkb + 1):
                if t < nkb:
                    da_a(t)
                if t >= 1:
                    da_e(t - 1)
            for bb in range(2):
                qb = 2 * g + bb
                tb = 2 * s + bb
                O1 = OB[0][bb]
                O2 = OB[1][bb]
                b.recip(fo_st[:, 0:1], O1[:, 256:257], r=[BK[4 + bb]], w=["fo_r1"])
                b.recip(fo_st[:, 1:2], O2[:, 256:257], r=[BK[6 + bb]], w=["fo_r2"])
                b.tt("dve", fo_st[:, 1:2], fo_st[:, 1:2], lam[:], ALU.mult, r=["fo_r2", "lam"], w=["fo_r2"])
                b.ts("dve", fo_t[:], O2[:, 0:256], fo_st[:, 1:2], None, ALU.mult, r=[BK[6 + bb], "fo_r2"],
                     w=["fo_t"])
                b.stt("dve", fo_oa[:], O1[:, 0:256], fo_st[:, 0:1], fo_t[:], ALU.mult, ALU.subtract,
                      r=[BK[4 + bb], "fo_r1", "fo_t"], w=["fo_oa"])
                b.act(fo_j[:], fo_oa[:], AF.Square, r=["fo_oa"], w=["fo_j", "s_ss"], accum_out=fo_st[:, 2:3])
                b.rstd(fo_st[:, 3:4], fo_st[:, 2:3], 1.0 / 256, "s")
                b.tt("dve", fo_st[:, 3:4], fo_st[:, 3:4], c1m, ALU.mult, r=["s_r", "sm"], w=["s_r"])
                b.stt("dve", fo_t[:], fo_oa[:], fo_st[:, 3:4], sgain, ALU.mult, ALU.mult, r=["fo_oa", "s_r", "sm"],
                      w=["fo_t"])
                uo = UAo[qb % 2]
                b.tt("dve", uo[:], fo_t[:], zs3[:, tb, :], ALU.mult, r=["fo_t", zk], w=[("UAo", qb % 2)])
                b.store(ua[qb * 128:(qb + 1) * 128, :], uo[:], r=[("UAo", qb % 2)], w=["ua_out"], sem=f"UAo{qb % 2}")

    load_weights(WDA, WSB, BK)
    OTB = [PB[6], PB[7]]
    for ci in range(nchunks):
        hb = HT[ci % 2]
        hk = ("HT", ci % 2)
        b.load(hb[:], hTv[:, :, ci * 512:(ci + 1) * 512], w=[hk], sem=f"HT{ci % 2}")
        qt = QT[ci % 2]
        qk_ = ("QT", ci % 2)
        zs = ZS[ci % 2]
        zk = ("ZS", ci % 2)
        zs2 = zs[:].rearrange("p (a c) -> p a c", a=2)
        for j in range(10):
            pb = PB[j % 2]
            pk = BK[j % 2]
            for kc in range(KC):
                b.mm(pb[:], WB[:, kc, j * 128:(j + 1) * 128], hb[:, kc, :], kc == 0, kc == KC - 1,
                     r=[hk] + wkeys(j * 128, (j + 1) * 128), w=[pk])
            if j < 2:
                b.ts("dve", qt[:, j, :], pb[:], inv, None, ALU.mult, r=[pk], w=[qk_])
            elif j < 4:
                b.copy("act", KT[:, j - 2, ci * 512:(ci + 1) * 512], pb[:], r=[pk],
                       w=[("KT", ci * 4 + t) for t in range(4)])
            elif j < 6:
                b.act(zs2[:, j - 4, :], pb[:], AF.Silu, r=[pk], w=[zk])
            else:
                so = SGo[j % 3]
                b.act(so[:], pb[:], AF.Sigmoid, r=[pk], w=[("SGo", j % 3)])
                b.store(sgT[(j - 6) * 128:(j - 5) * 128, ci * 512:(ci + 1) * 512], so[:], r=[("SGo", j % 3)],
                        w=["sg_out"], sem=f"SGo{j % 3}")
        for tb in range(4):
            blk = ci * 4 + tb
            pb = PB[tb % 2]
            pk = BK[tb % 2]
            for kc in range(KC):
                b.mm(pb[:, 0:256], hb[:, kc, tb * 128:(tb + 1) * 128], WB[:, kc, 1280:1536], kc == 0, kc == KC - 1,
                     r=[hk] + wkeys(1280, 1536), w=[pk])
            b.copy("act", V[:, blk, 0:256], pb[:, 0:256], r=[pk], w=[("V", blk)])
        nkb = 4 * ci + 4
        blocks = []
        for h in range(2):
            for kb in range(nkb - 1, -1, -1):
                blocks.append((h, kb, kb == nkb - 1, len(blocks)))

        def sb_a(blk):
            h, kb, first, xi = blk
            xb = PB[2 + xi % 4]
            xk = BK[2 + xi % 4]
            jd = kb - 4 * ci
            b.mm(xb[:], KT[:, h, kb * 128:(kb + 1) * 128], qt[:, h, :], True, jd < 0, r=[("KT", kb), qk_], w=[xk])
            if jd >= 0:
                b.mm(xb[:], ident[:], Mm[:, jd, :], False, True, r=["ident", "Mm"], w=[xk])
            b.act(T2[xi % 2][:], xb[:], AF.Exp, r=[xk], w=[f"T2_{xi % 2}"])
            b.act(SP[xi % 3][:], T2[xi % 2][:], AF.Ln, r=[f"T2_{xi % 2}"], w=[("SP", xi % 3)], bias=1.0)

        def sb_c(blk):
            h, kb, first, xi = blk
            xb = PB[2 + xi % 4]
            xk = BK[2 + xi % 4]
            sp = SP[xi % 3]
            spk = ("SP", xi % 3)
            b.mm(xb[:], ntri[:], sp[:], False, first, r=["ntri", spk], w=[xk], nocheck=True)
            if not first:
                b.mm(xb[:], nones[:], RACC[:], False, True, r=["nones", "RACC"], w=[xk], nocheck=True)
            if kb > 0:
                if first:
                    b.copy("dve", RACC[:], sp[:], r=[spk], w=["RACC"])
                else:
                    b.tt("dve", RACC[:], RACC[:], sp[:], ALU.add, r=[spk, "RACC"], w=["RACC"])
            b.act(PW[xi % 3][:], xb[:], AF.Exp, r=[xk], w=[("PW", xi % 3)])

        def sb_e(blk):
            h, kb, first, xi = blk
            b.mm(OTB[h][:], V[:, kb, h * 128:(h + 1) * 128], PW[xi % 3][:], first, kb == 0,
                 r=[("V", kb), ("PW", xi % 3)], w=[BK[6 + h]])

        nb_ = len(blocks)
        for t in range(nb_ + 2):
            if t < nb_:
                sb_a(blocks[t])
            if 0 <= t - 1 < nb_:
                sb_c(blocks[t - 1])
            if 0 <= t - 2 < nb_:
                sb_e(blocks[t - 2])
        for h in range(2):
            uo = UBo[h]
            b.tt("dve", uo[:], OTB[h][:], zs2[:, h, :], ALU.mult, r=[BK[6 + h], zk], w=[("UBo", h)])
            b.store(ubT[h * 128:(h + 1) * 128, ci * 512:(ci + 1) * 512], uo[:], r=[("UBo", h)], w=["ub_out"],
                    sem=f"UBo{h}")
    b.finish()
    return nc


def build_phase_c(merge=True, norm=True):
    nc = bass.Bass("TRN2", target_bir_lowering=False)
    b = B(nc)
    T = TPC
    NB = T // 128
    x = nc.dram_tensor("x", [T, D], F32, kind="ExternalInput").ap()
    ng = nc.dram_tensor("ng", [128, D], F32, kind="ExternalInput").ap()
    if merge:
        uaD = nc.dram_tensor("ua", [T, D], BF16, kind="ExternalInput").ap()
        ubD = nc.dram_tensor("ubT", [D, T], BF16, kind="ExternalInput").ap()
        sgaD = nc.dram_tensor("sgaT", [D, T], BF16, kind="ExternalInput").ap()
        sgbD = nc.dram_tensor("sgbT", [D, T], BF16, kind="ExternalInput").ap()
        waD = nc.dram_tensor("wa", [D, D], F32, kind="ExternalInput").ap()
        wbD = nc.dram_tensor("wb", [D, D], F32, kind="ExternalInput").ap()
        woD = nc.dram_tensor("wo", [D, D], F32, kind="ExternalInput").ap()
        xo = nc.dram_tensor("xo", [T, D], F32, kind="ExternalOutput").ap()
    if norm:
        hTo = nc.dram_tensor("hTo", [D, T], BF16, kind="ExternalOutput").ap()
        hTov = hTo.rearrange("(kc p) t -> p kc t", p=128)
    ident = make_ident(b)
    outs = []
    XB = [b.sb(f"XB{i}", [128, D], F32) for i in range(2)]
    if norm:
        NG = b.sb("NG", [128, D], F32)
        b.load(NG[:], ng, w=["NG"], sem="NG")
        junk = b.sb("junk", [128, D], BF16)
        nst = b.sb("nst", [128, 2], F32)
        HB = b.sb("HB", [128, D], BF16)
        HTO = [b.sb(f"HTO{i}", [128, KC, 512], BF16) for i in range(1)]
        PT = [b.ps(f"PT{i}", [128, 4, 128], BF16) for i in range(2)]
    if merge:
        BIG = b.sb("BIG", [128, 2, KC, T], BF16)
        YT = b.sb("YT", [128, KC, T], BF16)
        UAB = [b.sb(f"UAB{i}", [128, D], BF16) for i in range(1)]
        WS = [b.sb(f"WS{i}", [128, KC, 128], F32) for i in range(2)]
        WBF = [b.sb(f"WBF{i}", [128, KC, 128], BF16) for i in range(4)]
        SG = [b.sb(f"SG{i}", [128, 2, T], BF16) for i in range(2)]
        YA = [b.sb(f"YA{i}", [128, 512], F32) for i in range(2)]
        YB = [b.sb(f"YB{i}", [128, 512], F32) for i in range(2)]
        PA = [b.ps(f"PA{i}", [128, 512]) for i in range(4)]
        PTm = [b.ps(f"PTm{i}", [128, 4, 128], BF16) for i in range(2)]
        for kc in range(KC):
            b.load(BIG[:, 1, kc, :], ubD[kc * 128:(kc + 1) * 128, :], w=[("ubT", kc)], sem="ubT")
        for nb in range(NB):
            ub_ = UAB[0]
            b.load(ub_[:], uaD[nb * 128:(nb + 1) * 128, :], w=[("UAB", 0)], sem="UAB0")
            for q4 in range(4):
                pt = PTm[q4 % 2]
                for a in range(4):
                    kc = q4 * 4 + a
                    b.tr(pt[:, a, :], ub_[:, kc * 128:(kc + 1) * 128], ident[:], r=[("UAB", 0), "ident"],
                         w=[("PTm", q4 % 2, a)])
                b.copy("act" if q4 % 2 else "dve", BIG[:, 0, q4 * 4:(q4 + 1) * 4, nb * 128:(nb + 1) * 128], pt[:],
                       r=[("PTm", q4 % 2, a) for a in range(4)], w=[("uaT", nb)])
        UAT = [("uaT", nb) for nb in range(NB)]
        UBT = [("ubT", kc) for kc in range(KC)]
        for fo in range(KC):
            i2 = fo % 2
            for br, wD in enumerate((waD, wbD)):
                si = 2 * i2 + br
                b.load(WS[br][:], wD.rearrange("(kc p) n -> p kc n", p=128)[:, :, fo * 128:(fo + 1) * 128],
                       w=[("WS", br)], sem=f"WS{br}")
                b.copy("pool" if br else "dve", WBF[si][:], WS[br][:], r=[("WS", br)], w=[("WBF", si)])
            b.load(SG[i2][:, 0, :], sgaD[fo * 128:(fo + 1) * 128, :], w=[("SG", i2, 0)], sem=f"SGa{i2}")
            b.load(SG[i2][:, 1, :], sgbD[fo * 128:(fo + 1) * 128, :], w=[("SG", i2, 1)], sem=f"SGb{i2}")
            for th in range(T // 512):
                for br in range(2):
                    pa = PA[2 * (th % 2) + br]
                    pk = ("PA", 2 * (th % 2) + br)
                    si = 2 * i2 + br
                    for kc in range(KC):
                        b.mm(pa[:], WBF[si][:, kc, :], BIG[:, br, kc, th * 512:(th + 1) * 512], kc == 0, kc == KC - 1,
                             r=[("WBF", si)] + (UAT if br == 0 else UBT), w=[pk])
                ya = YA[th % 2]
                yk = ("YA", th % 2)
                b.tt("dve", ya[:], PA[2 * (th % 2)][:], SG[i2][:, 0, th * 512:(th + 1) * 512], ALU.mult,
                     r=[("PA", 2 * (th % 2)), ("SG", i2, 0)], w=[yk])
                yb = YB[th % 2]
                ybk = ("YB", th % 2)
                b.tt("dve", yb[:], PA[2 * (th % 2) + 1][:],
                     SG[i2][:, 1, th * 512:(th + 1) * 512], ALU.mult, r=[("PA", 2 * (th % 2) + 1), ("SG", i2, 1)],
                     w=[ybk])
                b.tt("pool", YT[:, fo, th * 512:(th + 1) * 512], yb[:], ya[:], ALU.add,
                     r=[ybk, yk], w=[("YT", fo, th)])
        YTK = [("YT", fo, th) for fo in range(KC) for th in range(T // 512)]
        WO = BIG[:].rearrange("p a k t -> p (a k t)").rearrange("p (k n) -> p k n", k=KC)
        for fo in range(KC):
            si = fo % 2
            b.load(WS[si][:], woD.rearrange("(kc p) n -> p kc n", p=128)[:, :, fo * 128:(fo + 1) * 128],
                   w=[("WS", si)], sem=f"WS{si}")
            b.copy("pool" if fo % 2 else "dve", WO[:, :, fo * 128:(fo + 1) * 128], WS[si][:], r=[("WS", si)],
                   w=[("WO", fo)] + UAT + UBT)
    for nb in range(NB):
        xb = XB[nb % 2]
        xk = ("XB", nb % 2)
        b.load(xb[:], x[nb * 128:(nb + 1) * 128, :], w=[xk], sem=f"XB{nb % 2}")
        if merge:
            for cg in range(4):
                pa = PA[cg]
                pk = ("PA", cg)
                for kc in range(KC):
                    b.mm(pa[:], YT[:, kc, nb * 128:(nb + 1) * 128], WO[:, kc, cg * 512:(cg + 1) * 512], kc == 0,
                         kc == KC - 1, r=YTK + [("WO", f) for f in range(cg * 4, cg * 4 + 4)], w=[pk])
                b.tt("dve", xb[:, cg * 512:(cg + 1) * 512], xb[:, cg * 512:(cg + 1) * 512], pa[:], ALU.add,
                     r=[xk, pk], w=[xk])
            b.store(xo[nb * 128:(nb + 1) * 128, :], xb[:], r=[xk], w=["xo_out"], sem=f"XO{nb % 2}")
            outs.append("xo_out")
        if norm:
            b.act(junk[:], xb[:], AF.Square, r=[xk], w=["junk", "n_ss"], accum_out=nst[:, 0:1])
            b.rstd(nst[:, 1:2], nst[:, 0:1], 1.0 / D, "n")
            b.stt("dve", HB[:], xb[:], nst[:, 1:2], NG[:], ALU.mult, ALU.mult, r=[xk, "n_r", "NG"], w=["HB"])
            ho = HTO[0]
            hok = ("HTO", 0)
            for q4 in range(4):
                pt = PT[q4 % 2]
                for a in range(4):
                    kc = q4 * 4 + a
                    b.tr(pt[:, a, :], HB[:, kc * 128:(kc + 1) * 128], ident[:], r=["HB", "ident"],
                         w=[("PT", q4 % 2, a)])
                b.copy("act" if q4 % 2 else "dve", ho[:, q4 * 4:(q4 + 1) * 4, (nb % 4) * 128:(nb % 4 + 1) * 128], pt[:],
                       r=[("PT", q4 % 2, a) for a in range(4)], w=[hok])
            if nb % 4 == 3:
                c4 = nb // 4
                b.store(hTov[:, :, c4 * 512:(c4 + 1) * 512], ho[:], r=[hok], w=["hT_out"], sem="HTO0")
                outs.append("hT_out")
    b.finish()
    return nc


_CACHE = {}


def _prog(name, fn):
    if name not in _CACHE:
        _CACHE[name] = fn()
    return _CACHE[name]


def _run(nc, maps):
    res = run_bass_kernel_spmd(nc, maps, core_ids=list(range(NCORES)))
    return res.results


def _rope_table():
    d = 128
    inv_freq = np.exp(-(np.arange(0, d, 2, dtype=np.float32) / d) * math.log(10000.0)).astype(np.float32)
    ang = np.arange(SEQ, dtype=np.float32)[:, None] * inv_freq[None, :]
    return np.concatenate([np.cos(ang), np.sin(ang)], axis=1).astype(np.float32)


def _w_cols(c):
    seg = 2048
    r = lambda s, a, n: np.arange(s * seg + a, s * seg + a + n)
    cols = [r(0, 256 * c, 256), r(1, 256 * c, 256), r(2, 256 * c, 256), r(3, 256 * c, 256),
            r(4, 256 * c, 256), r(5, 256 * c, 256), r(7, 256 * c, 256),
            r(8, 256 * c, 256), r(9, 256 * c, 256), r(6, 256 * c, 256)]
    return np.concatenate(cols)


def kernel(x, norm_gain, w_in, qk_q_gain, qk_k_gain, lambda_q1, lambda_k1, lambda_q2, lambda_k2,
           subln_gain, w_branch_a, w_branch_b, w_out):
    bf = ml_dtypes.bfloat16
    rep = lambda v: np.ascontiguousarray(np.broadcast_to(np.asarray(v, np.float32)[None, :], (128, v.shape[-1])))
    xs = [np.ascontiguousarray(x[0, c * TPC:(c + 1) * TPC, :]) for c in range(NCORES)]
    cs = _rope_table()
    pn = _prog("n", lambda: build_phase_c(merge=False, norm=True))
    pab = _prog("ab", build_phase_ab)
    pc = _prog("c", lambda: build_phase_c(merge=True, norm=True))
    pcl = _prog("cl", lambda: build_phase_c(merge=True, norm=False))
    res = _run(pn, [{"x": xs[c], "ng": rep(norm_gain[0])} for c in range(NCORES)])
    hT = np.concatenate([res[c]["hTo"] for c in range(NCORES)], axis=1)
    for l in range(DEPTH):
        li = 0.8 - 0.6 * math.exp(-0.3 * l)
        small = np.concatenate([rep(qk_q_gain[l]), rep(qk_k_gain[l]), rep(lambda_q1[l]), rep(lambda_k1[l]),
                                rep(lambda_q2[l]), rep(lambda_k2[l]), rep(subln_gain[l]),
                                np.full((128, 1), li, np.float32), np.full((128, 1), 1.0 - li, np.float32)], axis=1)
        maps = [{"hT": hT, "w": np.ascontiguousarray(w_in[l][:, _w_cols(c)]), "cs": cs, "small": small}
                for c in range(NCORES)]
        res = _run(pab, maps)
        ua = np.concatenate([res[c]["ua"] for c in range(NCORES)], axis=1)
        ubT = np.concatenate([res[c]["ubT"] for c in range(NCORES)], axis=0)
        sgaT = np.concatenate([res[c]["sgT"][0:256] for c in range(NCORES)], axis=0)
        sgbT = np.concatenate([res[c]["sgT"][256:512] for c in range(NCORES)], axis=0)
        last = l == DEPTH - 1
        maps = []
        for c in range(NCORES):
            t0, t1 = c * TPC, (c + 1) * TPC
            maps.append({"x": xs[c], "ng": rep(norm_gain[min(l + 1, DEPTH - 1)]),
                         "ua": np.ascontiguousarray(ua[t0:t1]), "ubT": np.ascontiguousarray(ubT[:, t0:t1]),
                         "sgaT": np.ascontiguousarray(sgaT[:, t0:t1]), "sgbT": np.ascontiguousarray(sgbT[:, t0:t1]),
                         "wa": w_branch_a[l], "wb": w_branch_b[l], "wo": w_out[l]})
        res = _run(pcl if last else pc, maps)
        xs = [res[c]["xo"] for c in range(NCORES)]
        if not last:
            hT = np.concatenate([res[c]["hTo"] for c in range(NCORES)], axis=1)
    return np.concatenate(xs, axis=0)[None].astype(np.float32)
```
